# Optimizing a Trainium2 kernel written in Bass

```python
import jax, jax.numpy as jnp
from jax import lax
import numpy as np

D_MODEL = 1024
BATCH = 8
SEQ = 2048
DEPTH = 1
DEC_BATCH = 32
DEC_SEQ = 8
PAST_LEN = 8192
PAGE_SIZE = 128

HEAD_DIM = 64
D_MIX = D_MODEL
SB_HEADS = (D_MIX // 2) // HEAD_DIM
NSA_HEADS = (D_MIX // 2) // HEAD_DIM
NSA_KV_HEADS = 2
NSA_GROUP = NSA_HEADS // NSA_KV_HEADS
CMP_BLOCK = 32
CMP_HIDDEN = 256
SEL_BLOCK = 64
N_SEL = 16
WINDOW = 512
PLE_DIM = 256
SB_Q_BLOCK = 128
NSA_Q_BLOCK = 64
EPS = 1e-6
FORCED_SCORE = 1e3
NEG_INF = -1e30
SB_W = SB_HEADS * HEAD_DIM
NSA_W = NSA_HEADS * HEAD_DIM
KV_W = NSA_KV_HEADS * HEAD_DIM
SPLITS = (SB_W, SB_W, SB_W, SB_W, NSA_W, KV_W, KV_W, KV_W, KV_W, KV_W, KV_W, 3 * NSA_HEADS, NSA_W)
N_IN = sum(SPLITS)

kernel_name = "hymba_stickbreak_nsa_decode_step"


def rms_norm(x, g):
    xf = x.astype(jnp.float32)
    y = xf * lax.rsqrt(jnp.mean(xf * xf, axis=-1, keepdims=True) + EPS)
    return (y * g.astype(jnp.float32)).astype(x.dtype)


def alibi_slopes():
    h = np.arange(1, NSA_HEADS + 1, dtype=np.float32)
    slopes = np.power(np.float32(2.0), -8.0 * h / NSA_HEADS).astype(np.float32)
    return jnp.asarray(slopes, dtype=jnp.float32).reshape(NSA_KV_HEADS, NSA_GROUP)


def masked_softmax(s, mask):
    s = jnp.where(mask, s, NEG_INF)
    m = jnp.max(s, axis=-1, keepdims=True)
    e = jnp.where(mask, jnp.exp(s - m), 0.0)
    den = jnp.sum(e, axis=-1, keepdims=True)
    return e / jnp.where(den > 0, den, 1.0)


def sweep_query_blocks(fn, qs, q_pos, block):
    T = q_pos.shape[0]
    if T <= block or T % block:
        return fn(*qs, q_pos)
    nb = T // block

    def split(a):
        return a.reshape(a.shape[0], nb, block, *a.shape[2:]).swapaxes(0, 1)

    xs = (tuple(split(a) for a in qs), q_pos.reshape(nb, block))
    out = lax.map(lambda z: fn(*z[0], z[1]), xs)
    return out.swapaxes(0, 1).reshape(out.shape[1], T, *out.shape[3:])


def gather_pages(pool, page_table):
    rows = pool[page_table]
    return rows.reshape(rows.shape[0], rows.shape[1] * rows.shape[2], *rows.shape[3:])


def stick_breaking(q, q_pos, k, v, k_pos):
    z = jnp.einsum('bqhd,bkhd->bhqk', q, k).astype(jnp.float32) * (HEAD_DIM ** -0.5)
    before = (k_pos[None, :] < q_pos[:, None])[None, None]
    log_beta = jnp.where(before, jax.nn.log_sigmoid(z), -jnp.inf)
    log_keep = jnp.where(before, jax.nn.log_sigmoid(-z), 0.0)
    log_keep_after = lax.cumsum(log_keep, axis=3, reverse=True) - log_keep
    a = jnp.exp(log_beta + log_keep_after)
    return jnp.einsum('bhqk,bkhd->bqhd', a, v.astype(jnp.float32)).astype(q.dtype)


def compress(rows, pe, w1, w2):
    B, L, G, D = rows.shape
    blocks = rows.reshape(B, L // CMP_BLOCK, CMP_BLOCK, G, D) + pe[:, None, :]
    flat = blocks.transpose(0, 1, 3, 2, 4).reshape(B, L // CMP_BLOCK, G, CMP_BLOCK * D)
    return jax.nn.silu(flat @ w1) @ w2


def nsa_attend(q, gates, q_pos, kc, vc, k_sel, v_sel, k_win, v_win, kw_start):
    f32 = jnp.float32
    B, Tq = q.shape[:2]
    G, R, D = NSA_KV_HEADS, NSA_GROUP, HEAD_DIM
    scale = D ** -0.5
    slopes = alibi_slopes()
    qg = q.reshape(B, Tq, G, R, D)
    t = q_pos[:, None]

    n_cmp = kc.shape[1]
    c_end = jnp.arange(n_cmp, dtype=jnp.int32) * CMP_BLOCK + (CMP_BLOCK - 1)
    s = jnp.einsum('bqgrd,bcgd->bqgrc', qg, kc).astype(f32) * scale
    s = s - slopes[None, None, :, :, None] * (t - c_end[None, :]).astype(f32)[None, :, None, None, :]
    p_c = masked_softmax(s, (c_end[None, :] <= t)[None, :, None, None, :])
    o_c = jnp.einsum('bqgrc,bcgd->bqgrd', p_c, vc.astype(f32))

    n_blk = k_sel.shape[1] // SEL_BLOCK
    imp = p_c.sum(axis=3).reshape(B, Tq, G, n_blk, SEL_BLOCK // CMP_BLOCK).sum(-1)
    blk = jnp.arange(n_blk, dtype=jnp.int32)[None, :]
    cur = (q_pos // SEL_BLOCK)[:, None]
    forced = ((blk == 0) | (blk == cur) | (blk == cur - 1))[None, :, None, :]
    future = (blk > cur)[None, :, None, :]
    score = jnp.where(forced, FORCED_SCORE, jnp.where(future, -1.0, imp))
    n_top = min(N_SEL, n_blk)
    _, idx = lax.top_k(score, n_top)
    idx = idx.transpose(0, 2, 1, 3)
    kb = k_sel.reshape(B, n_blk, SEL_BLOCK, G, D).transpose(0, 3, 1, 2, 4)
    vb = v_sel.reshape(B, n_blk, SEL_BLOCK, G, D).transpose(0, 3, 1, 2, 4)
    take = jax.vmap(jax.vmap(lambda a, i: a[i]))
    gk = take(kb, idx)
    gv = take(vb, idx)
    kpos = (idx[..., None] * SEL_BLOCK + jnp.arange(SEL_BLOCK, dtype=jnp.int32))[:, :, :, None]
    tq = q_pos[None, None, :, None, None, None]
    qt = qg.transpose(0, 2, 1, 3, 4)
    s = jnp.einsum('bgqrd,bgqnld->bgqrnl', qt, gk).astype(f32) * scale
    s = s - slopes[None, :, None, :, None, None] * (tq - kpos).astype(f32)
    s = s.reshape(B, G, Tq, R, n_top * SEL_BLOCK)
    p_s = masked_softmax(s, (kpos <= tq).reshape(B, G, Tq, 1, n_top * SEL_BLOCK))
    o_s = jnp.einsum('bgqrk,bgqkd->bgqrd', p_s,
                     gv.reshape(B, G, Tq, n_top * SEL_BLOCK, D).astype(f32)).transpose(0, 2, 1, 3, 4)

    n_band = WINDOW + Tq
    start = q_pos[0] - WINDOW - kw_start
    kwb = lax.dynamic_slice_in_dim(k_win, start, n_band, axis=1)
    vwb = lax.dynamic_slice_in_dim(v_win, start, n_band, axis=1)
    wpos = q_pos[0] - WINDOW + jnp.arange(n_band, dtype=jnp.int32)
    s = jnp.einsum('bqgrd,bkgd->bqgrk', qg, kwb).astype(f32) * scale
    s = s - slopes[None, None, :, :, None] * (t - wpos[None, :]).astype(f32)[None, :, None, None, :]
    wmask = (wpos[None, :] <= t) & (wpos[None, :] > t - WINDOW) & (wpos[None, :] >= 0)
    p_w = masked_softmax(s, wmask[None, :, None, None, :])
    o_w = jnp.einsum('bqgrk,bkgd->bqgrd', p_w, vwb.astype(f32))

    g = gates.reshape(B, Tq, G, R, 3).astype(f32)
    o = g[..., 0:1] * o_c + g[..., 1:2] * o_s + g[..., 2:3] * o_w
    return o.reshape(B, Tq, NSA_HEADS, D).astype(q.dtype)


def project(x, norm_g, w_in, q_g, ks_g, kw_g):
    B, T, _ = x.shape
    h = rms_norm(x, norm_g)
    z = h @ w_in
    split_at = [int(c) for c in np.cumsum(SPLITS)[:-1]]
    (q_a, k_a, v_a, gate_a, q_b, kc, vc, ks, vs, kw, vw, br, gate_b) = jnp.split(z, split_at, axis=-1)

    def heads(a, n):
        return a.reshape(B, T, n, HEAD_DIM)

    G = NSA_KV_HEADS
    sb_rows = jnp.stack([heads(k_a, SB_HEADS), heads(v_a, SB_HEADS)], axis=2)
    nsa_rows = jnp.stack([heads(kc, G), heads(vc, G), rms_norm(heads(ks, G), ks_g), heads(vs, G)], axis=2)
    win_rows = jnp.stack([rms_norm(heads(kw, G), kw_g), heads(vw, G)], axis=2)
    q_b = rms_norm(heads(q_b, NSA_HEADS), q_g)
    br = jax.nn.sigmoid(br.reshape(B, T, NSA_HEADS, 3))
    return heads(q_a, SB_HEADS), gate_a, q_b, br, gate_b, sb_rows, nsa_rows, win_rows


def decoder_layer(x, p, start_pos, past_sb, past_nsa, past_win, win_keep, lw):
    (norm_g, w_in, q_g, kc_g, ks_g, kw_g, pe_k, w1_k, w2_k, pe_v, w1_v, w2_v, w_out, w_ple, w_ple_gate) = lw
    B, T, _ = x.shape
    q_a, gate_a, q_b, br, gate_b, sb_rows, nsa_rows, win_rows = project(x, norm_g, w_in, q_g, ks_g, kw_g)
    q_pos = start_pos + jnp.arange(T, dtype=jnp.int32)

    sb = jnp.concatenate([past_sb, sb_rows], axis=1)
    sb_k, sb_v = sb[:, :, 0], sb[:, :, 1]
    k_pos = jnp.arange(sb.shape[1], dtype=jnp.int32)
    o_sb = sweep_query_blocks(lambda qb, pb: stick_breaking(qb, pb, sb_k, sb_v, k_pos), (q_a,), q_pos, SB_Q_BLOCK)

    nsa = jnp.concatenate([past_nsa, nsa_rows], axis=1)
    L = nsa.shape[1]
    l_pad = -(-L // SEL_BLOCK) * SEL_BLOCK
    nsa = jnp.pad(nsa, ((0, 0), (0, l_pad - L), (0, 0), (0, 0), (0, 0)))
    kc = rms_norm(compress(nsa[:, :, 0], pe_k, w1_k, w2_k), kc_g)
    vc = compress(nsa[:, :, 1], pe_v, w1_v, w2_v)
    k_sel, v_sel = nsa[:, :, 2], nsa[:, :, 3]
    win = jnp.concatenate([past_win, win_rows], axis=1)
    win_p = jnp.pad(win, ((0, 0), (WINDOW - past_win.shape[1], 0), (0, 0), (0, 0), (0, 0)))
    k_win, v_win = win_p[:, :, 0], win_p[:, :, 1]
    kw_start = start_pos - WINDOW
    o_nsa = sweep_query_blocks(
        lambda qb, gb, pb: nsa_attend(qb, gb, pb, kc, vc, k_sel, v_sel, k_win, v_win, kw_start),
        (q_b, br), q_pos, NSA_Q_BLOCK)

    u = jnp.concatenate([o_sb.reshape(B, T, SB_W) * jax.nn.silu(gate_a),
                         o_nsa.reshape(B, T, NSA_W) * jax.nn.silu(gate_b)], axis=-1)
    h = x + u @ w_out
    y = h + jax.nn.sigmoid(h @ w_ple_gate) * (p @ w_ple)
    return y, sb_rows, nsa_rows, win[:, win.shape[1] - win_keep:]


def setup_inputs(seed: int = 0) -> dict:
    key = jax.random.key(seed)
    ks = jax.random.split(key, 23)
    n_pages = PAST_LEN // PAGE_SIZE
    n_used = DEC_BATCH * n_pages
    n_pool = n_used + n_used // 4
    w_buf = min(WINDOW, PAST_LEN)
    f32 = jnp.float32

    def nrm(k, shape, s=1.0):
        return jax.random.normal(k, shape, f32) * s

    def gain(k, shape):
        return 1.0 + 0.01 * jax.random.normal(k, shape, f32)

    page_table = jax.random.permutation(ks[7], n_pool)[:n_used].reshape(DEC_BATCH, n_pages).astype(jnp.int32)
    return {
        "x_prompt": nrm(ks[0], (BATCH, SEQ, D_MODEL)),
        "x_sample": nrm(ks[1], (DEC_BATCH, DEC_SEQ, D_MODEL)),
        "p_prompt": nrm(ks[2], (DEPTH, BATCH, SEQ, PLE_DIM)),
        "p_sample": nrm(ks[3], (DEPTH, DEC_BATCH, DEC_SEQ, PLE_DIM)),
        "cache_sb": nrm(ks[4], (DEPTH, n_pool, PAGE_SIZE, 2, SB_HEADS, HEAD_DIM)),
        "cache_nsa": nrm(ks[5], (DEPTH, n_pool, PAGE_SIZE, 4, NSA_KV_HEADS, HEAD_DIM)),
        "state_win": nrm(ks[6], (DEPTH, DEC_BATCH, w_buf, 2, NSA_KV_HEADS, HEAD_DIM)),
        "page_table": page_table,
        "norm_g": gain(ks[8], (DEPTH, D_MODEL)),
        "w_in": nrm(ks[9], (DEPTH, D_MODEL, N_IN), D_MODEL ** -0.5),
        "q_norm_g": gain(ks[10], (DEPTH, HEAD_DIM)),
        "kcmp_norm_g": gain(ks[11], (DEPTH, HEAD_DIM)),
        "ksel_norm_g": gain(ks[12], (DEPTH, HEAD_DIM)),
        "kwin_norm_g": gain(ks[13], (DEPTH, HEAD_DIM)),
        "cmp_pos_k": nrm(ks[14], (DEPTH, CMP_BLOCK, HEAD_DIM), 0.1),
        "cmp_w1_k": nrm(ks[15], (DEPTH, CMP_BLOCK * HEAD_DIM, CMP_HIDDEN), (CMP_BLOCK * HEAD_DIM) ** -0.5),
        "cmp_w2_k": nrm(ks[16], (DEPTH, CMP_HIDDEN, HEAD_DIM), CMP_HIDDEN ** -0.5),
        "cmp_pos_v": nrm(ks[17], (DEPTH, CMP_BLOCK, HEAD_DIM), 0.1),
        "cmp_w1_v": nrm(ks[18], (DEPTH, CMP_BLOCK * HEAD_DIM, CMP_HIDDEN), (CMP_BLOCK * HEAD_DIM) ** -0.5),
        "cmp_w2_v": nrm(ks[19], (DEPTH, CMP_HIDDEN, HEAD_DIM), CMP_HIDDEN ** -0.5),
        "w_out": nrm(ks[20], (DEPTH, D_MIX, D_MODEL), D_MIX ** -0.5),
        "w_ple": nrm(ks[21], (DEPTH, PLE_DIM, D_MODEL), PLE_DIM ** -0.5),
        "w_ple_gate": nrm(ks[22], (DEPTH, D_MODEL, D_MODEL), D_MODEL ** -0.5),
    }


def reference(x_prompt, x_sample, p_prompt, p_sample, cache_sb, cache_nsa, state_win, page_table,
              norm_g, w_in, q_norm_g, kcmp_norm_g, ksel_norm_g, kwin_norm_g,
              cmp_pos_k, cmp_w1_k, cmp_w2_k, cmp_pos_v, cmp_w1_v, cmp_w2_v,
              w_out, w_ple, w_ple_gate):
    B, T = x_prompt.shape[0], x_prompt.shape[1]
    dt = x_prompt.dtype
    empty_sb = jnp.zeros((B, 0, 2, SB_HEADS, HEAD_DIM), dt)
    empty_nsa = jnp.zeros((B, 0, 4, NSA_KV_HEADS, HEAD_DIM), dt)
    empty_win = jnp.zeros((B, 0, 2, NSA_KV_HEADS, HEAD_DIM), dt)
    y_p, y_s = x_prompt, x_sample
    sb_p, sb_s, nsa_p, nsa_s, win_p, win_s = [], [], [], [], [], []
    for i in range(DEPTH):
        lw = (norm_g[i], w_in[i], q_norm_g[i], kcmp_norm_g[i], ksel_norm_g[i], kwin_norm_g[i],
              cmp_pos_k[i], cmp_w1_k[i], cmp_w2_k[i], cmp_pos_v[i], cmp_w1_v[i], cmp_w2_v[i],
              w_out[i], w_ple[i], w_ple_gate[i])
        y_p, a, b, c = decoder_layer(y_p, p_prompt[i], 0, empty_sb, empty_nsa, empty_win, min(WINDOW, T), lw)
        sb_p.append(a); nsa_p.append(b); win_p.append(c)
        y_s, a, b, c = decoder_layer(y_s, p_sample[i], PAST_LEN,
                                     gather_pages(cache_sb[i], page_table),
                                     gather_pages(cache_nsa[i], page_table),
                                     state_win[i], state_win.shape[2], lw)
        sb_s.append(a); nsa_s.append(b); win_s.append(c)
    return (y_p, y_s, jnp.stack(sb_p), jnp.stack(sb_s), jnp.stack(nsa_p), jnp.stack(nsa_s), jnp.stack(win_p), jnp.stack(win_s))
```

```python
import contextlib
import numpy as np
import concourse.bass as bass
import concourse.mybir as mybir
from concourse.bass_utils import run_bass_kernel_spmd

F32 = mybir.dt.float32
BF16 = mybir.dt.bfloat16
I32 = mybir.dt.int32
AF = mybir.ActivationFunctionType
ALU = mybir.AluOpType
AX = mybir.AxisListType

NCORE = 8
D = 1024
T = 2048
NT = 16
SB = 4
SQ = 8
NPAGE = 64
PAST = 8192
NPOOL = 2560
EPS = 1e-6
NEG = -30000.0
N_IN = 3864
SLOPES = [2.0 ** (-(h + 1)) for h in range(8)]

GROUPS = {"sbq": (0, 512), "sbk": (512, 512), "sbv": (1024, 512), "sbg": (1536, 512),
          "nq": (2048, 512), "ncs": (2560, 512), "nwb": (3072, 280), "ng": (3352, 512)}


import os
MAXE = int(os.environ.get('MAXE', '100000000'))


class Sched:
    def __init__(self, nc, stack):
        self.nc = nc
        self.stack = stack
        self.engs = {'pe': nc.tensor, 'act': nc.scalar, 'dve': nc.vector, 'pool': nc.gpsimd, 'sp': nc.sync}
        self.cnt = {}
        self.sems = {}
        self.seen = {e: {} for e in self.engs}
        self.lastw = {}
        self.rds = {}
        self.ninstr = {e: 0 for e in self.engs}

    def sem(self, name):
        if name not in self.sems:
            self.sems[name] = self.stack.enter_context(self.nc.semaphore(name))
            self.cnt[name] = 0
        return self.sems[name]

    def emit(self, engine, fn, reads=(), writes=(), dma=None, inc=None):
        self.total = getattr(self, 'total', 0) + 1
        if self.total > MAXE:
            return None
        if self.total == MAXE:
            print('LAST EMIT', engine, reads, writes, dma, flush=True)
        deps = set()
        for r in reads:
            if r in self.lastw:
                deps.add(self.lastw[r])
        for w in writes:
            if w in self.lastw:
                deps.add(self.lastw[w])
            for t in self.rds.get(w, ()):
                deps.add(t)
        own = 'E_' + engine
        waits = {}
        for (s, v) in deps:
            if s == own and engine == 'pe':
                continue
            if self.seen[engine].get(s, 0) >= v:
                continue
            waits[s] = max(waits.get(s, 0), v)
        eng = self.engs[engine]
        for s, v in waits.items():
            self.seen[engine][s] = v
            eng.wait_ge(self.sems[s], v)
        if dma is None:
            sname, inc = own, 1
        else:
            sname, inc = 'D_' + dma, (16 if inc is None else inc)
        sh = self.sem(sname)
        self.cnt[sname] += inc
        tok = (sname, self.cnt[sname])
        ins = fn(eng)
        ins.then_inc(sh, inc)
        self.ninstr[engine] += 1
        for w in writes:
            self.lastw[w] = tok
            self.rds[w] = []
        for r in reads:
            if r not in writes:
                self.rds.setdefault(r, []).append(tok)
        return tok

    def barrier(self):
        for e, eng in self.engs.items():
            for s, c in self.cnt.items():
                if c > 0 and self.seen[e].get(s, 0) < c and not (s == 'E_pe' and e == 'pe'):
                    eng.wait_ge(self.sems[s], c)
                    self.seen[e][s] = c
        self.lastw.clear()
        self.rds.clear()


def host_consts():
    c = {}
    p = np.arange(128)
    c["ident"] = np.eye(128, dtype=np.float32)
    c["tri"] = (p[:, None] >= p[None, :]).astype(np.float32)
    c["cotri"] = (p[:, None] < p[None, :]).astype(np.float32)
    c["onesm"] = np.ones((128, 128), np.float32)
    c["sneg"] = np.where(p[:, None] >= p[None, :], NEG, 0.0).astype(np.float32)
    c["cneg"] = np.where(p[:, None] > p[None, :], NEG, 0.0).astype(np.float32)
    c["bneg"] = np.where(p[:, None] <= p[None, :], NEG, 0.0).astype(np.float32)
    q8 = np.arange(8)
    c["sneg8"] = np.tile(np.where(p[:, None] >= q8[None, :], NEG, 0.0), (1, 8)).astype(np.float32)
    c["cneg8"] = np.where(p[:, None] > q8[None, :], NEG, 0.0).astype(np.float32)
    c["bneg8"] = np.where(p[:, None] <= q8[None, :], NEG, 0.0).astype(np.float32)
    sl = np.array(SLOPES, np.float64)
    t = np.arange(T)
    aq = np.zeros((4, NT, 8, 128), np.float64)
    tq = t.reshape(NT, 128)
    aq[0] = -8.0 * sl[None, :, None] * (256 * (tq // 256))[:, None, :]
    aq[1] = -8.0 * sl[None, :, None] * (tq % 256)[:, None, :]
    aq[2] = 8.0 * sl[None, :, None]
    aq[3] = 8.0 * sl[None, :, None]
    c["aq_p"] = aq.astype(np.float32)
    ts = PAST + q8
    aqs = np.zeros((4, 8, 8), np.float64)
    aqs[0] = -8.0 * sl[:, None] * (256 * (ts // 256))[None, :]
    aqs[1] = -8.0 * sl[:, None] * (ts % 256)[None, :]
    aqs[2] = 8.0 * sl[:, None]
    aqs[3] = 8.0 * sl[:, None]
    c["aq_s"] = np.tile(aqs[:, None, :, :], (1, SB, 1, 1)).astype(np.float32)
    kp = np.arange(PAST + 8)
    ak = np.stack([np.ones_like(kp), np.ones_like(kp), 256 * (kp // 256), kp % 256]).astype(np.float32)
    c["ak"] = ak
    cend = 32 * np.arange(64) + 31
    dist = t[:, None] - cend[None, :]
    b = -sl[None, :, None] * dist[:, None, :]
    b = np.where(dist[:, None, :] >= 0, b, -1e30)
    c["biasc_p"] = b.reshape(NT, 128, 8 * 64).astype(np.float32)
    cend_s = 32 * np.arange(256) + 31
    dist_s = ts[:, None] - cend_s[None, :]
    c["biasc_s"] = (-sl[None, :, None] * dist_s[:, None, :]).reshape(8, 8 * 256).astype(np.float32)
    blk = np.arange(32)
    cur = t // 64
    forced = (blk[None, :] == 0) | (blk[None, :] == cur[:, None]) | (blk[None, :] == cur[:, None] - 1)
    future = blk[None, :] > cur[:, None]
    c["mulm_p"] = (~(forced | future)).astype(np.float32).reshape(NT, 128, 32)
    c["addm_p"] = np.where(forced, 1e3, np.where(future, -1.0, 0.0)).astype(np.float32).reshape(NT, 128, 32)
    blk_s = np.arange(128)
    forced_s = (blk_s == 0) | (blk_s == 127)
    c["mulm_s"] = np.tile((~forced_s).astype(np.float32)[None, :], (8, 1))
    c["addm_s"] = np.tile(np.where(forced_s, 1e3, 0.0).astype(np.float32)[None, :], (8, 1))
    kk = np.arange(T)
    c["esel_p"] = (kk[None, :] // 64 == blk[:, None]).astype(np.float32)
    kk = np.arange(PAST)
    c["esel_s"] = (kk[None, :] // 64 == blk_s[:, None]).astype(np.float32)
    c["iota_p"] = p.astype(np.float32).reshape(128, 1)
    return c


CONST_SHAPES = None


def build(npool=NPOOL, allgather=True, debug=False, skip=()):
    nc = bass.Bass("TRN2", target_bir_lowering=False)
    consts = host_consts()

    def din(name, shape, dt=F32):
        return nc.dram_tensor(name, list(shape), dt, kind="ExternalInput").ap()

    def dout(name, shape, dt=F32):
        return nc.dram_tensor(name, list(shape), dt, kind="ExternalOutput").ap()

    x_p = din("x_p", [T, D]); p_p = din("p_p", [T, 256])
    x_s = din("x_s", [SB * SQ, D]); p_s = din("p_s", [SB * SQ, 256])
    state_win = din("state_win", [SB, 512, 256])
    ptab = din("ptab", [SB, NPAGE], I32)
    g_b = din("g_b", [128, D])
    w_in = din("w_in", [128, 8, N_IN])
    qg_b = din("qg_b", [128, 64]); kcg_b = din("kcg_b", [128, 64])
    ksg_b = din("ksg_b", [128, 64]); kwg_b = din("kwg_b", [128, 64])
    w1k_d = din("w1k", [128, 32, 256]); w1v_d = din("w1v", [128, 32, 256])
    w2k_d = din("w2k", [128, 2, 64]); w2v_d = din("w2v", [128, 2, 64])
    pek_d = din("pekT", [64, 32]); pev_d = din("pevT", [64, 32])
    wout_d = din("w_out", [128, 8, D]); wpg_d = din("w_pg", [128, 8, D]); wple_d = din("w_ple", [128, 2, D])
    cd = {k: din("c_" + k, v.shape) for k, v in consts.items()}
    if allgather:
        sh = npool // NCORE
        csb_sh = din("csb_sh", [8, sh * 128, 128]); cns_sh = din("cns_sh", [4, sh * 128, 128])
        ag_list = []
        sb_chunks = []; nsa_chunks = []
        for (nm, src, nchunk, lst) in (("csb", csb_sh, 8, sb_chunks), ("cns", cns_sh, 4, nsa_chunks)):
            for j in range(nchunk):
                loc_t = nc.dram_tensor(f"{nm}_loc{j}", [sh * 128, 128], F32)
                full_t = nc.dram_tensor(f"{nm}_full{j}", [npool * 128, 128], F32)
                ag_list.append((src[j], loc_t, full_t))
                lst.append((full_t.ap(), 128 * j, 128))
    else:
        cache_sb = din("cache_sb", [npool * 128, 1024])
        cache_nsa = din("cache_nsa", [npool * 128, 512])
        sb_chunks = [(cache_sb, 0, 1024)]
        nsa_chunks = [(cache_nsa, 0, 512)]

    y_p = dout("y_p", [T, D]); y_s = dout("y_s", [SB * SQ, D])
    sb_p = dout("sb_p", [T, 1024]); sb_s = dout("sb_s", [SB * SQ, 1024])
    nsa_p = dout("nsa_p", [T, 512]); nsa_s = dout("nsa_s", [SB * SQ, 512])
    win_p = dout("win_p", [512, 256]); win_s = dout("win_s", [SB, 512, 256])
    if debug:
        u_pd = dout("u_pd", [T, D]); u_sd = dout("u_sd", [SB * SQ, D])

    with contextlib.ExitStack() as top:
        S = Sched(nc, top)
        E = S.emit

        uniq = [0]

        def sb_t(st, name, shape, dt):
            uniq[0] += 1
            return st.enter_context(nc.sbuf_tensor(f"s{uniq[0]}_" + name, list(shape), dt))

        def ps_t(st, name, shape, dt):
            return st.enter_context(nc.psum_tensor("p_" + name, list(shape), dt))

        pf = [ps_t(top, f"pf{i}", [128, 512], F32) for i in range(6)]
        pb = [ps_t(top, f"pb{i}", [128, 1024], BF16) for i in range(2)]

        if allgather:
            rg = [list(range(NCORE))]
            nrow = sh * 128 * 128 // 512
            for (src, loc_t, full_t) in ag_list:
                srcv = src.rearrange("(a r) c -> a (r c)", r=4)
                locv = loc_t.ap().rearrange("(a r) c -> a (r c)", r=4)
                for r0 in range(0, nrow, 1024):
                    r1 = min(nrow, r0 + 1024)
                    E('pool', lambda e: e.dma_start(out=locv[r0:r1, :], in_=srcv[r0:r1, :]), writes=['agloc'], dma='ag0')
                S.barrier()
                E('pool', lambda e: e.collective_compute("AllGather", ALU.bypass, replica_groups=rg,
                                                         ins=[loc_t.ap().opt()], outs=[full_t.ap().opt()]),
                  reads=['agloc'], writes=['cache_sb', 'cache_nsa'], dma='ag1', inc=1)
            S.barrier()

        cidx = [0]

        def load_const(dst_ap, src_ap, res, cast=False):
            cidx[0] += 1
            E('pool' if cast else 'sp', lambda e: e.dma_start(out=dst_ap, in_=src_ap), writes=[res], dma=f'c{cidx[0] % 4}')

        identf = sb_t(top, "identf", [128, 128], F32)
        ident = sb_t(top, "ident", [128, 128], BF16)
        tri = sb_t(top, "tri", [128, 128], BF16); cotri = sb_t(top, "cotri", [128, 128], BF16)
        onesm = sb_t(top, "onesm", [128, 128], BF16); zerom = sb_t(top, "zerom", [128, 128], BF16)
        sneg = sb_t(top, "sneg", [128, 128], BF16); cneg = sb_t(top, "cneg", [128, 128], BF16)
        bneg = sb_t(top, "bneg", [128, 128], BF16)
        sneg8 = sb_t(top, "sneg8", [128, 64], BF16); cneg8 = sb_t(top, "cneg8", [128, 8], BF16)
        bneg8 = sb_t(top, "bneg8", [128, 8], BF16)
        iotap = sb_t(top, "iotap", [128, 1], F32)
        qg = sb_t(top, "qg", [128, 64], F32); kcg = sb_t(top, "kcg", [128, 64], F32)
        ksg = sb_t(top, "ksg", [128, 64], F32); kwg = sb_t(top, "kwg", [128, 64], F32)
        load_const(identf[:], cd["ident"], 'identf')
        for (tl, nm) in [(ident, "ident"), (tri, "tri"), (cotri, "cotri"), (onesm, "onesm"), (sneg, "sneg"),
                         (cneg, "cneg"), (bneg, "bneg"), (sneg8, "sneg8"), (cneg8, "cneg8"), (bneg8, "bneg8")]:
            load_const(tl[:], cd[nm], nm, cast=True)
        E('dve', lambda e: e.memset(zerom[:], 0.0), writes=['zerom'])
        zrhs = sb_t(top, "zrhs", [128, 512], BF16)
        E('dve', lambda e: e.memset(zrhs[:], 0.0), writes=['zrhs'])
        load_const(iotap[:], cd["iota_p"], 'iotap')
        load_const(qg[:], qg_b, 'qg'); load_const(kcg[:], kcg_b, 'kcg')
        load_const(ksg[:], ksg_b, 'ksg'); load_const(kwg[:], kwg_b, 'kwg')

        QTa_s = sb_t(top, "QTa_s", [128, 4, SB * SQ], BF16)
        KTa_s = sb_t(top, "KTa_s", [128, 4, SB * SQ], BF16)
        Va_s = sb_t(top, "Va_s", [SQ, SB, 512], BF16)
        ga_s = sb_t(top, "ga_s", [SQ, SB, 512], BF16)
        gb_s = sb_t(top, "gb_s", [SQ, SB, 512], BF16)
        QTb_s = sb_t(top, "QTb_s", [68, SB, 8, SQ], BF16)
        KTsel_s = sb_t(top, "KTsel_s", [68, SB, 2, SQ], BF16)
        KTwin_s = sb_t(top, "KTwin_s", [68, SB, 2, SQ], BF16)
        Vsel_s = sb_t(top, "Vsel_s", [SQ, SB, 2, 65], BF16)
        Vwin_s = sb_t(top, "Vwin_s", [SQ, SB, 2, 65], BF16)
        gates_s = sb_t(top, "gates_s", [SQ, SB, 24], F32)
        ua_s = sb_t(top, "ua_s", [SQ, SB, 512], BF16)
        ub_s = sb_t(top, "ub_s", [SQ, SB, 512], BF16)
        load_const(QTb_s[64:68, :, :, :], cd["aq_s"], 'QTb_s', cast=True)
        for b in range(SB):
            for g in range(2):
                load_const(KTsel_s[64:68, b, g, :], cd["ak"][:, PAST:PAST + 8], 'KTsel_s', cast=True)
                load_const(KTwin_s[64:68, b, g, :], cd["ak"][:, PAST:PAST + 8], 'KTwin_s', cast=True)
        E('dve', lambda e: e.memset(Vsel_s[:, :, :, 64:65], 1.0), writes=['Vsel_s'])
        E('dve', lambda e: e.memset(Vwin_s[:, :, :, 64:65], 1.0), writes=['Vwin_s'])
        S.barrier()

        ua = sb_t(top, "ua", [128, NT, 512], BF16)
        ub = sb_t(top, "ub", [128, NT, 512], BF16)

        tiles = [(128, x_p[i * 128:(i + 1) * 128, :], i * 128) for i in range(NT)]
        tiles += [(SQ, x_s[b * SQ:(b + 1) * SQ, :], T + b * SQ) for b in range(SB)]
        NCOLS = T + SB * SQ

        def rstd_chain(stt, R, n, res, scale):
            E('act', lambda e: e.activation(out=stt[0:R, n:2 * n], in_=stt[0:R, 0:n], func=AF.Ln, scale=scale, bias=EPS),
              reads=[res], writes=[res])
            E('act', lambda e: e.activation(out=stt[0:R, 2 * n:3 * n], in_=stt[0:R, n:2 * n], func=AF.Exp, scale=-0.5),
              reads=[res], writes=[res])

        def sigmoid_from(z_ap, tmp_ap, R, zres, tres):
            E('act', lambda e: e.activation(out=tmp_ap, in_=z_ap, func=AF.Exp, scale=-1.0), reads=zres, writes=[tres])
            E('dve', lambda e: e.tensor_scalar(tmp_ap, tmp_ap, 1.0, None, op0=ALU.add), reads=[tres], writes=[tres])
            E('dve', lambda e: e.reciprocal(tmp_ap, tmp_ap), reads=[tres], writes=[tres])

        def make_hT(ph1):
            hT = sb_t(ph1, "hT", [128, 8, NCOLS], BF16)
            if 'inp' in skip:
                return hT
            with contextlib.ExitStack() as st:
                xt = [sb_t(st, f"xt{i}", [128, D], F32) for i in range(2)]
                gb_t = sb_t(st, "gbt", [128, D], F32)
                load_const(gb_t[:], g_b, 'gbt')
                S.barrier()
                junk = sb_t(st, "junk", [128, D], BF16)
                hn = [sb_t(st, f"hn{i}", [128, D], BF16) for i in range(2)]
                ss = [sb_t(st, f"ss{i}", [128, 4], F32) for i in range(2)]
                for ti, (R, xsrc, c0) in enumerate(tiles):
                    p = ti % 2
                    pT = pb[p][:, :].rearrange("p (k c) -> p k c", k=8)
                    E('sp', lambda e: e.dma_start(out=xt[p][0:R, :], in_=xsrc), writes=[f'xt{p}'], dma=f'xt{p}')
                    E('act', lambda e: e.activation(out=junk[0:R, :], in_=xt[p][0:R, :], func=AF.Square,
                                                    accum_out=ss[p][0:R, 0:1]),
                      reads=[f'xt{p}'], writes=['junk', f'ss{p}'])
                    rstd_chain(ss[p], R, 1, f'ss{p}', 1.0 / D)
                    E('dve', lambda e: e.scalar_tensor_tensor(out=hn[p][0:R, :], in0=xt[p][0:R, :], scalar=ss[p][0:R, 2:3],
                                                              in1=gb_t[0:R, :], op0=ALU.mult, op1=ALU.mult),
                      reads=[f'xt{p}', f'ss{p}', 'gbt'], writes=[f'hn{p}'])
                    for k in range(8):
                        E('pe', lambda e: e.transpose(pT[:, k, 0:R], hn[p][0:R, k * 128:(k + 1) * 128], ident[0:R, 0:R]),
                          reads=[f'hn{p}', 'ident'], writes=[f'pb{p}'])
                    E('dve', lambda e: e.tensor_copy(hT[:, :, c0:c0 + R], pT[:, :, 0:R]), reads=[f'pb{p}'], writes=['hT'])
            S.barrier()
            return hT

        if True:
            def inproj(hT, groups, dst):
                if 'inp' in skip:
                    return
                with contextlib.ExitStack() as st:
                    wbf = [sb_t(st, f"wbf{i}", [128, 8, 512], BF16) for i in range(2)]
                    rows = [sb_t(st, f"rows{i}", [128, 512], F32) for i in range(2)]
                    tm = [sb_t(st, f"tm{i}", [128, 512], BF16) for i in range(2)]
                    sq = sb_t(st, "sq", [128, 512], F32)
                    stt = [sb_t(st, f"stt{i}", [128, 24], F32) for i in range(2)]
                    step = 0
                    for gi, gname in enumerate(groups):
                        gc0, gw = GROUPS[gname]
                        wp = gi % 2
                        E('pool', lambda e: e.dma_start(out=wbf[wp][:, :, 0:gw], in_=w_in[:, :, gc0:gc0 + gw]),
                          writes=[f'wbf{wp}'], dma=f'wbf{wp}')
                        for ti, (R, xsrc, c0) in enumerate(tiles):
                            zpar = step % 2
                            step += 1
                            z = pf[zpar]
                            zr = [f'pf{zpar}']
                            pbn = f'pb{zpar}'
                            for k in range(8):
                                E('pe', lambda e: e.matmul(z[0:R, 0:gw], hT[:, k, c0:c0 + R], wbf[wp][:, k, 0:gw],
                                                           start=(k == 0), stop=(k == 7)),
                                  reads=['hT', f'wbf{wp}'], writes=zr)
                            is_p = ti < NT
                            b = ti - NT
                            r0 = ti * 128 if is_p else b * SQ
                            rw = rows[zpar]; rwn = f'rows{zpar}'
                            tmb = tm[zpar]; tmn = f'tm{zpar}'
                            s4 = stt[zpar]; s4n = f'stt{zpar}'

                            def store(dst_ap, src_ap):
                                E('pool', lambda e: e.dma_start(out=dst_ap, in_=src_ap), reads=[rwn], dma=f'o_rows{zpar}')

                            def tr_pairs(dst_ap, dres):
                                pT = pb[zpar][:, :].rearrange("p (k c) -> p k c", k=8)
                                for j in range(4):
                                    E('pe', lambda e: e.transpose(pT[:, j, 0:R], tmb[0:R, j * 128:(j + 1) * 128], ident[0:R, 0:R]),
                                      reads=[tmn, 'ident'], writes=[pbn])
                                E('act', lambda e: e.activation(out=dst_ap, in_=pT[:, 0:4, 0:R], func=AF.Copy),
                                  reads=[pbn], writes=[dres])

                            def norm_pair(col0, gain, gname_):
                                E('act', lambda e: e.activation(out=sq[0:R, 0:128], in_=z[0:R, col0:col0 + 128], func=AF.Square),
                                  reads=zr, writes=['sq'])
                                E('dve', lambda e: e.tensor_reduce(out=s4[0:R, 0:2], in_=sq[0:R, 0:128].rearrange("p (g d) -> p g d", g=2),
                                                                   axis=AX.X, op=ALU.add), reads=['sq'], writes=[s4n])
                                rstd_chain(s4, R, 2, s4n, 1.0 / 64)
                                for g in range(2):
                                    E('dve', lambda e: e.scalar_tensor_tensor(
                                        out=rw[0:R, col0 + 64 * g:col0 + 64 + 64 * g], in0=z[0:R, col0 + 64 * g:col0 + 64 + 64 * g],
                                        scalar=s4[0:R, 4 + g:5 + g], in1=gain[0:R, :], op0=ALU.mult, op1=ALU.mult),
                                      reads=zr + [s4n, gname_], writes=[rwn])

                            def tr_heads64(src_ap_fn, nheads, dst_ap, dres):
                                pT = pb[zpar][:, :].rearrange("p (k c) -> p k c", k=8)
                                for h in range(nheads):
                                    E('pe', lambda e: e.transpose(pT[0:64, h, 0:R], src_ap_fn(h), ident[0:R, 0:R]),
                                      reads=[tmn, 'ident'], writes=[pbn])
                                E('act', lambda e: e.activation(out=dst_ap, in_=pT[0:64, 0:nheads, 0:R], func=AF.Copy),
                                  reads=[pbn], writes=[dres])

                            if gname == "sbq":
                                E('dve', lambda e: e.tensor_copy(tmb[0:R, :], z[0:R, :]), reads=zr, writes=[tmn])
                                if is_p:
                                    tr_pairs(dst["QTa"][:, :, c0:c0 + R], 'QTa')
                                else:
                                    tr_pairs(QTa_s[:, :, b * SQ:(b + 1) * SQ], 'QTa_s')
                            elif gname == "sbk":
                                E('act', lambda e: e.activation(out=rw[0:R, :], in_=z[0:R, :], func=AF.Copy), reads=zr, writes=[rwn])
                                store((sb_p if is_p else sb_s)[r0:r0 + R, 0:512], rw[0:R, :])
                                E('dve', lambda e: e.tensor_copy(tmb[0:R, :], rw[0:R, :]), reads=[rwn], writes=[tmn])
                                if is_p:
                                    tr_pairs(dst["KTa"][:, :, c0:c0 + R], 'KTa')
                                else:
                                    tr_pairs(KTa_s[:, :, b * SQ:(b + 1) * SQ], 'KTa_s')
                            elif gname == "sbv":
                                E('act', lambda e: e.activation(out=rw[0:R, :], in_=z[0:R, :], func=AF.Copy), reads=zr, writes=[rwn])
                                store((sb_p if is_p else sb_s)[r0:r0 + R, 512:1024], rw[0:R, :])
                                if is_p:
                                    E('dve', lambda e: e.tensor_copy(dst["Va"][:, ti, :], rw[0:R, :]), reads=[rwn], writes=['Va'])
                                else:
                                    E('dve', lambda e: e.tensor_copy(Va_s[:, b, :], rw[0:R, :]), reads=[rwn], writes=['Va_s'])
                            elif gname in ("sbg", "ng"):
                                sigmoid_from(z[0:R, :], sq[0:R, :], R, zr, 'sq')
                                if gname == "sbg":
                                    o_ap, ores = (dst["ga"][:, ti, :], 'ga') if is_p else (ga_s[:, b, :], 'ga_s')
                                else:
                                    o_ap, ores = (dst["gb"][:, ti, :], 'gb') if is_p else (gb_s[:, b, :], 'gb_s')
                                E('dve', lambda e: e.tensor_tensor(out=o_ap, in0=z[0:R, :], in1=sq[0:R, :], op=ALU.mult),
                                  reads=zr + ['sq'], writes=[ores])
                            elif gname == "nq":
                                E('act', lambda e: e.activation(out=sq[0:R, :], in_=z[0:R, :], func=AF.Square), reads=zr, writes=['sq'])
                                E('dve', lambda e: e.tensor_reduce(out=s4[0:R, 0:8], in_=sq[0:R, :].rearrange("p (h d) -> p h d", h=8),
                                                                   axis=AX.X, op=ALU.add), reads=['sq'], writes=[s4n])
                                rstd_chain(s4, R, 8, s4n, 1.0 / 64)
                                E('dve', lambda e: e.tensor_tensor(out=sq[0:R, :].rearrange("p (h d) -> p h d", h=8),
                                                                   in0=z[0:R, :].rearrange("p (h d) -> p h d", h=8),
                                                                   in1=s4[0:R, 16:24].unsqueeze(2).broadcast_to([R, 8, 64]), op=ALU.mult),
                                  reads=zr + [s4n], writes=['sq'])
                                E('dve', lambda e: e.tensor_tensor(out=tmb[0:R, :].rearrange("p (h d) -> p h d", h=8),
                                                                   in0=sq[0:R, :].rearrange("p (h d) -> p h d", h=8),
                                                                   in1=qg[0:R, :].unsqueeze(1).broadcast_to([R, 8, 64]), op=ALU.mult),
                                  reads=['sq', 'qg'], writes=[tmn])
                                if is_p:
                                    tr_heads64(lambda h: tmb[0:R, h * 64:(h + 1) * 64], 8, dst["QTb"][0:64, ti, :, :], 'QTb')
                                else:
                                    tr_heads64(lambda h: tmb[0:R, h * 64:(h + 1) * 64], 8, QTb_s[0:64, b, :, :], 'QTb_s')
                            elif gname == "ncs":
                                E('act', lambda e: e.activation(out=rw[0:R, 0:256], in_=z[0:R, 0:256], func=AF.Copy), reads=zr, writes=[rwn])
                                E('act', lambda e: e.activation(out=rw[0:R, 384:512], in_=z[0:R, 384:512], func=AF.Copy), reads=zr, writes=[rwn])
                                norm_pair(256, ksg, 'ksg')
                                store((nsa_p if is_p else nsa_s)[r0:r0 + R, :], rw[0:R, :])
                                E('dve', lambda e: e.tensor_copy(tmb[0:R, :], rw[0:R, :]), reads=[rwn], writes=[tmn])
                                pT = pb[zpar][:, :].rearrange("p (k c) -> p k c", k=8)
                                if is_p:
                                    for j in range(2):
                                        E('pe', lambda e: e.transpose(pT[:, j, 0:R], tmb[0:R, j * 128:(j + 1) * 128], ident[0:R, 0:R]),
                                          reads=[tmn, 'ident'], writes=[pbn])
                                    E('act', lambda e: e.activation(out=dst["XT"][:, :, c0:c0 + R], in_=pT[:, 0:2, 0:R], func=AF.Copy),
                                      reads=[pbn], writes=['XT'])
                                    tr_heads64(lambda g: tmb[0:R, 256 + g * 64:320 + g * 64], 2, dst["KTsel"][0:64, :, c0:c0 + R], 'KTsel')
                                    E('dve', lambda e: e.tensor_copy(dst["Vsel"][:, ti, :, 0:64],
                                                                     rw[0:R, 384:512].rearrange("p (g d) -> p g d", g=2)),
                                      reads=[rwn], writes=['Vsel'])
                                else:
                                    tr_heads64(lambda g: tmb[0:R, 256 + g * 64:320 + g * 64], 2, KTsel_s[0:64, b, :, :], 'KTsel_s')
                                    E('dve', lambda e: e.tensor_copy(Vsel_s[:, b, :, 0:64],
                                                                     rw[0:R, 384:512].rearrange("p (g d) -> p g d", g=2)),
                                      reads=[rwn], writes=['Vsel_s'])
                            elif gname == "nwb":
                                E('act', lambda e: e.activation(out=rw[0:R, 128:256], in_=z[0:R, 128:256], func=AF.Copy), reads=zr, writes=[rwn])
                                norm_pair(0, kwg, 'kwg')
                                if is_p:
                                    if ti >= NT - 4:
                                        store(win_p[(ti - (NT - 4)) * 128:(ti - (NT - 4) + 1) * 128, :], rw[0:R, 0:256])
                                else:
                                    store(win_s[b, 504:512, :], rw[0:R, 0:256])
                                E('dve', lambda e: e.tensor_copy(tmb[0:R, 0:128], rw[0:R, 0:128]), reads=[rwn], writes=[tmn])
                                if is_p:
                                    tr_heads64(lambda g: tmb[0:R, g * 64:64 + g * 64], 2, dst["KTwin"][0:64, :, c0:c0 + R], 'KTwin')
                                    E('dve', lambda e: e.tensor_copy(dst["Vwin"][:, ti, :, 0:64],
                                                                     rw[0:R, 128:256].rearrange("p (g d) -> p g d", g=2)),
                                      reads=[rwn], writes=['Vwin'])
                                    g_ap, gres = dst["gates"][:, ti, :], 'gates'
                                else:
                                    tr_heads64(lambda g: tmb[0:R, g * 64:64 + g * 64], 2, KTwin_s[0:64, b, :, :], 'KTwin_s')
                                    E('dve', lambda e: e.tensor_copy(Vwin_s[:, b, :, 0:64],
                                                                     rw[0:R, 128:256].rearrange("p (g d) -> p g d", g=2)),
                                      reads=[rwn], writes=['Vwin_s'])
                                    g_ap, gres = gates_s[:, b, :], 'gates_s'
                                sigmoid_from(z[0:R, 256:280], g_ap, R, zr, gres)

            with contextlib.ExitStack() as sbst:
                hT = make_hT(sbst)
                QTa = sb_t(sbst, "QTa", [128, 4, T], BF16)
                KTa = sb_t(sbst, "KTa", [128, 4, T], BF16)
                Va = sb_t(sbst, "Va", [128, NT, 512], BF16)
                ga = sb_t(sbst, "ga", [128, NT, 512], BF16)
                inproj(hT, ["sbq", "sbk", "sbv", "sbg"], dict(QTa=QTa, KTa=KTa, Va=Va, ga=ga))
                S.barrier()
                with contextlib.ExitStack() as st:
                    e32 = [sb_t(st, f"e32{i}", [128, 512], F32) for i in range(2)]
                    spb = [sb_t(st, f"spb{i}", [128, 512], BF16) for i in range(2)]
                    e2 = [sb_t(st, f"e2{i}", [128, 512], F32) for i in range(2)]
                    ab = [sb_t(st, f"ab{i}", [128, 512], BF16) for i in range(2)]
                    step = 0

                    def sb_chain(A, An, Cacc, Cn, K, n_, par, c0=0):
                        sl_ = slice(c0, c0 + n_)
                        E('act', lambda e: e.activation(out=e32[par][0:K, sl_], in_=A[0:K, sl_], func=AF.Exp, scale=0.125),
                          reads=[An], writes=[f'e32{par}'])
                        E('act', lambda e: e.activation(out=spb[par][0:K, sl_], in_=e32[par][0:K, sl_], func=AF.Ln, scale=1.0, bias=1.0),
                          reads=[f'e32{par}'], writes=[f'spb{par}'])
                        E('pe', lambda e: e.matmul(Cacc[0:K, sl_], tri[0:K, 0:K], spb[par][0:K, sl_], start=False, stop=True,
                                                   skip_group_check=True),
                          reads=[f'spb{par}', 'tri'], writes=[Cn])
                        E('act', lambda e: e.activation(out=e2[par][0:K, sl_], in_=Cacc[0:K, sl_], func=AF.Exp, scale=-1.0),
                          reads=[Cn], writes=[f'e2{par}'])
                        E('dve', lambda e: e.tensor_tensor(out=ab[par][0:K, sl_], in0=e32[par][0:K, sl_], in1=e2[par][0:K, sl_], op=ALU.mult),
                          reads=[f'e32{par}', f'e2{par}'], writes=[f'ab{par}'])

                    for h in (range(8) if 'sbp' not in skip else ()):
                        j, base = h // 2, 64 * (h % 2)
                        for Q in range(4):
                            hq = (h * 4 + Q) % 2
                            Cacc, Cn = pf[2 + hq], f'pf{2 + hq}'
                            O, On = pf[4 + hq], f'pf{4 + hq}'
                            Ov = O[:, 0:256].rearrange("p (j d) -> p j d", j=4)
                            E('pe', lambda e: e.matmul(Cacc[:, :], zerom[:, :], zrhs[:, :], start=True, stop=True, skip_group_check=True),
                              reads=['zerom', 'zrhs'], writes=[Cn])
                            E('pe', lambda e: e.matmul(O[:, 0:256], zerom[:, :], zrhs[:, 0:256], start=True, stop=True, skip_group_check=True),
                              reads=['zerom', 'zrhs'], writes=[On])
                            for kt in range(4 * Q + 3, -1, -1):
                                par = step % 2
                                step += 1
                                A, An = pf[par], f'pf{par}'
                                jd = kt - 4 * Q
                                c0 = max(0, jd) * 128
                                n_ = 512 - c0
                                sl_ = slice(c0, 512)
                                E('pe', lambda e: e.matmul(A[:, sl_], KTa[base:base + 64, j, kt * 128:(kt + 1) * 128],
                                                           QTa[base:base + 64, j, Q * 512 + c0:Q * 512 + 512], start=True, stop=(jd < 0)),
                                  reads=['KTa', 'QTa'], writes=[An])
                                if jd >= 0:
                                    E('pe', lambda e: e.matmul(A[:, c0:c0 + 128], ident[:, :], sneg[:, :], start=False, stop=True),
                                      reads=['ident', 'sneg'], writes=[An])
                                sb_chain(A, An, Cacc, Cn, 128, n_, par, c0)
                                E('pe', lambda e: e.matmul(Cacc[:, sl_], cotri[:, :], spb[par][:, sl_], start=False, stop=True,
                                                           skip_group_check=True),
                                  reads=[f'spb{par}', 'cotri'], writes=[Cn])
                                for jj in range(max(0, jd), 4):
                                    E('pe', lambda e: e.matmul(Ov[:, jj, :], ab[par][:, jj * 128:(jj + 1) * 128],
                                                               Va[:, kt, h * 64:(h + 1) * 64],
                                                               start=False, stop=True, skip_group_check=True),
                                      reads=[f'ab{par}', 'Va'], writes=[On])
                            E('dve', lambda e: e.tensor_tensor(out=ua[:, 4 * Q:4 * Q + 4, h * 64:(h + 1) * 64], in0=Ov,
                                                               in1=ga[:, 4 * Q:4 * Q + 4, h * 64:(h + 1) * 64], op=ALU.mult),
                              reads=[On, 'ga'], writes=['ua'])

                    Qbd = sb_t(st, "Qbd", [128, 4, 64], BF16)
                    idxf = sb_t(st, "idxf", [128, NPAGE], F32)
                    idxi = sb_t(st, "idxi", [128, NPAGE], I32)
                    ptb = sb_t(st, "ptb", [128, NPAGE], I32)
                    kv = [sb_t(st, f"kv{i}", [128, 1024], BF16) for i in range(2)]
                    KTp = [sb_t(st, f"KTp{i}", [128, 4, 128], BF16) for i in range(2)]
                    for b in (range(SB) if 'sbs' not in skip else ()):
                        E('dve', lambda e: e.memset(Qbd[:], 0.0), writes=['Qbd'])
                        for j in range(4):
                            for hh in range(2):
                                h = 2 * j + hh
                                E('dve', lambda e: e.tensor_copy(Qbd[64 * hh:64 * hh + 64, j, h * 8:(h + 1) * 8],
                                                                 QTa_s[64 * hh:64 * hh + 64, j, b * SQ:(b + 1) * SQ]),
                                  reads=['QTa_s'], writes=['Qbd'])
                        E('sp', lambda e: e.dma_start(out=ptb[:], in_=ptab[b:b + 1, :].partition_broadcast(128)), writes=['ptb'], dma='ptb')
                        E('dve', lambda e: e.tensor_copy(idxf[:], ptb[:]), reads=['ptb'], writes=['idxf'])
                        E('dve', lambda e: e.tensor_scalar(idxf[:], idxf[:], 128.0, iotap[:, 0:1], op0=ALU.mult, op1=ALU.add),
                          reads=['idxf', 'iotap'], writes=['idxf'])
                        E('dve', lambda e: e.tensor_copy(idxi[:], idxf[:]), reads=['idxf'], writes=['idxi'])
                        hq = b % 2
                        Cacc, Cn = pf[2 + hq], f'pf{2 + hq}'
                        O, On = pf[4 + hq], f'pf{4 + hq}'
                        Cnew, Cnn = pf[2 + (1 - hq)], f'pf{2 + (1 - hq)}'
                        par = step % 2
                        step += 1
                        A, An = pf[par], f'pf{par}'
                        for j in range(4):
                            E('pe', lambda e: e.matmul(A[0:8, 0:64], KTa_s[:, j, b * SQ:(b + 1) * SQ], Qbd[:, j, :], start=(j == 0), stop=False),
                              reads=['KTa_s', 'Qbd'], writes=[An])
                        E('pe', lambda e: e.matmul(A[0:8, 0:64], ident[0:8, 0:8], sneg8[0:8, :], start=False, stop=True),
                          reads=['ident', 'sneg8'], writes=[An])
                        E('pe', lambda e: e.matmul(Cnew[0:8, 0:64], zerom[0:8, 0:8], sneg8[0:8, :], start=True, stop=True, skip_group_check=True),
                          reads=['zerom'], writes=[Cnn])
                        sb_chain(A, An, Cnew, Cnn, 8, 64, par)
                        E('pe', lambda e: e.matmul(Cacc[:, 0:64], onesm[0:8, :], spb[par][0:8, 0:64], start=True, stop=True, skip_group_check=True),
                          reads=[f'spb{par}', 'onesm'], writes=[Cn])
                        E('pe', lambda e: e.matmul(O[0:8, 0:512], zerom[:, 0:8], zrhs[:, 0:512], start=True, stop=True, skip_group_check=True),
                          reads=['zerom', 'zrhs'], writes=[On])
                        for h in range(8):
                            E('pe', lambda e: e.matmul(O[0:8, h * 64:(h + 1) * 64], ab[par][0:8, h * 8:(h + 1) * 8],
                                                       Va_s[0:8, b, h * 64:(h + 1) * 64], start=False, stop=True, skip_group_check=True),
                              reads=[f'ab{par}', 'Va_s'], writes=[On])
                        for pg in range(NPAGE - 1, -1, -1):
                            par = step % 2
                            step += 1
                            A, An = pf[par], f'pf{par}'
                            kvt, kvn = kv[par], f'kv{par}'
                            for (cap, clo, cw) in sb_chunks:
                                E('pool', lambda e: e.indirect_dma_start(out=kvt[:, clo:clo + cw], out_offset=None, in_=cap,
                                                                         in_offset=bass.IndirectOffsetOnAxis(ap=idxi[:, pg:pg + 1], axis=0)),
                                  reads=['idxi', 'cache_sb'], writes=[kvn], dma=kvn)
                            pT = pb[par][:, :].rearrange("p (k c) -> p k c", k=8)
                            for j in range(4):
                                E('pe', lambda e: e.transpose(pT[:, j, :], kvt[:, j * 128:(j + 1) * 128], ident[:, :]),
                                  reads=[kvn, 'ident'], writes=[f'pb{par}'])
                            E('dve', lambda e: e.tensor_copy(KTp[par][:], pT[:, 0:4, :]), reads=[f'pb{par}'], writes=[f'KTp{par}'])
                            for j in range(4):
                                E('pe', lambda e: e.matmul(A[:, 0:64], KTp[par][:, j, :], Qbd[:, j, :], start=(j == 0), stop=(j == 3)),
                                  reads=[f'KTp{par}', 'Qbd'], writes=[An])
                            sb_chain(A, An, Cacc, Cn, 128, 64, par)
                            E('pe', lambda e: e.matmul(Cacc[:, 0:64], cotri[:, :], spb[par][:, 0:64], start=False, stop=True, skip_group_check=True),
                              reads=[f'spb{par}', 'cotri'], writes=[Cn])
                            for h in range(8):
                                E('pe', lambda e: e.matmul(O[0:8, h * 64:(h + 1) * 64], ab[par][:, h * 8:(h + 1) * 8],
                                                           kvt[:, 512 + h * 64:512 + (h + 1) * 64], start=False, stop=True,
                                                           skip_group_check=True),
                                  reads=[f'ab{par}', kvn], writes=[On])
                        E('dve', lambda e: e.tensor_tensor(out=ua_s[:, b, :], in0=O[0:8, :], in1=ga_s[:, b, :], op=ALU.mult),
                          reads=[On, 'ga_s'], writes=['ua_s'])
                S.barrier()

        if True:
            nsast = contextlib.ExitStack()
            QTb = sb_t(nsast, "QTb", [68, NT, 8, 128], BF16)
            gbp = sb_t(nsast, "gbp", [128, NT, 512], BF16)
            XT = sb_t(nsast, "XT", [128, 2, T], BF16)
            KTsel = sb_t(nsast, "KTsel", [68, 2, T], BF16)
            KTwin = sb_t(nsast, "KTwin", [68, 2, T], BF16)
            Vsel = sb_t(nsast, "Vsel", [128, NT, 2, 65], BF16)
            Vwin = sb_t(nsast, "Vwin", [128, NT, 2, 65], BF16)
            gates = sb_t(nsast, "gates", [128, NT, 24], F32)
            load_const(QTb[64:68, :, :, :], cd["aq_p"], 'QTb', cast=True)
            for g in range(2):
                load_const(KTsel[64:68, g, :], cd["ak"][:, 0:T], 'KTsel', cast=True)
                load_const(KTwin[64:68, g, :], cd["ak"][:, 0:T], 'KTwin', cast=True)
            E('dve', lambda e: e.memset(Vsel[:, :, :, 64:65], 1.0), writes=['Vsel'])
            E('dve', lambda e: e.memset(Vwin[:, :, :, 64:65], 1.0), writes=['Vwin'])
            S.barrier()
            with contextlib.ExitStack() as hst:
                hT = make_hT(hst)
                inproj(hT, ["nq", "ncs", "nwb", "ng"], dict(QTb=QTb, gb=gbp, XT=XT, KTsel=KTsel, KTwin=KTwin, Vsel=Vsel, Vwin=Vwin, gates=gates))
                S.barrier()
        S.barrier()

        def nsa_phase(which):
          with contextlib.ExitStack() as nst:
            NCMP = 64 if which == 'prompt' else 256
            NBLK = NCMP // 2

            def make_compress(cs):
                W1 = [sb_t(cs, f"W1{i}", [128, 32, 256], BF16) for i in range(2)]
                W2 = [sb_t(cs, f"W2{i}", [128, 2, 64], BF16) for i in range(2)]
                peT = [sb_t(cs, f"peT{i}", [64, 32], BF16) for i in range(2)]
                b1 = sb_t(cs, "b1", [128, 2, 2], F32)
                load_const(W1[0][:], w1k_d, 'W1', cast=True); load_const(W1[1][:], w1v_d, 'W1', cast=True)
                load_const(W2[0][:], w2k_d, 'W2', cast=True); load_const(W2[1][:], w2v_d, 'W2', cast=True)
                load_const(peT[0][:], pek_d, 'peT', cast=True); load_const(peT[1][:], pev_d, 'peT', cast=True)
                S.barrier()
                for kind in range(2):
                    for jc in range(2):
                        for l in range(32):
                            E('pe', lambda e: e.matmul(pf[0][:, kind * 2 + jc:kind * 2 + jc + 1], W1[kind][0:64, l, jc * 128:(jc + 1) * 128],
                                                       peT[kind][:, l:l + 1], start=(l == 0), stop=(l == 31), skip_group_check=True),
                              reads=['W1', 'peT'], writes=['pf0'])
                E('dve', lambda e: e.tensor_copy(b1[:].rearrange("p k j -> p (k j)"), pf[0][:, 0:4]), reads=['pf0'], writes=['b1'])

                hb = sb_t(cs, "hb", [128, 2 * NCMP], F32)
                hsg = sb_t(cs, "hsg", [128, 2 * NCMP], F32)
                HT = sb_t(cs, "HT", [128, 2, NCMP], BF16)
                kcn = sb_t(cs, "kcn", [128, 128], BF16)
                ktmp = sb_t(cs, "ktmp", [128, 128], F32)
                cst = sb_t(cs, "cst", [128, 8], F32)

                def compress(xt_fn, ncmp, kcT_dst, vc_dst, kres, vres):
                    nct = (ncmp + 127) // 128
                    for kind in range(2):
                        for g in range(2):
                            Hps = pf[1][:, 0:2 * ncmp].rearrange("p (j c) -> p j c", j=2)
                            for jc in range(2):
                                for l in range(32):
                                    E('pe', lambda e: e.matmul(Hps[:, jc, :], W1[kind][64 * g:64 * g + 64, l, jc * 128:(jc + 1) * 128],
                                                               xt_fn(kind, g, l), start=(l == 0), stop=(l == 31), skip_group_check=True),
                                      reads=['W1', 'XTsrc'], writes=['pf1'])
                            hbv = hb[:, 0:2 * ncmp].rearrange("p (j c) -> p j c", j=2)
                            hsv = hsg[:, 0:2 * ncmp].rearrange("p (j c) -> p j c", j=2)
                            for jc in range(2):
                                E('dve', lambda e: e.tensor_scalar(hbv[:, jc, :], Hps[:, jc, :], b1[:, kind, jc:jc + 1], None, op0=ALU.add),
                                  reads=['pf1', 'b1'], writes=['hb'])
                            sigmoid_from(hb[:, 0:2 * ncmp], hsg[:, 0:2 * ncmp], 128, ['hb'], 'hsg')
                            E('dve', lambda e: e.tensor_tensor(out=HT[:, :, 0:ncmp], in0=hbv, in1=hsv, op=ALU.mult),
                              reads=['hb', 'hsg'], writes=['HT'])
                            for ct in range(nct):
                                cn_ = min(128, ncmp - ct * 128)
                                for jc in range(2):
                                    E('pe', lambda e: e.matmul(pf[0][0:cn_, 0:64], HT[:, jc, ct * 128:ct * 128 + cn_], W2[kind][:, jc, :],
                                                               start=(jc == 0), stop=(jc == 1)),
                                      reads=['HT', 'W2'], writes=['pf0'])
                                if kind == 0:
                                    E('act', lambda e: e.activation(out=ktmp[0:cn_, 0:64], in_=pf[0][0:cn_, 0:64], func=AF.Square,
                                                                    accum_out=cst[0:cn_, 0:1]), reads=['pf0'], writes=['ktmp', 'cst'])
                                    rstd_chain(cst, cn_, 1, 'cst', 1.0 / 64)
                                    E('dve', lambda e: e.scalar_tensor_tensor(out=kcn[0:cn_, 0:64], in0=pf[0][0:cn_, 0:64], scalar=cst[0:cn_, 2:3],
                                                                              in1=kcg[0:cn_, :], op0=ALU.mult, op1=ALU.mult),
                                      reads=['pf0', 'cst', 'kcg'], writes=['kcn'])
                                    E('pe', lambda e: e.transpose(pb[0][0:64, 0:cn_], kcn[0:cn_, 0:64], ident[0:cn_, 0:cn_]),
                                      reads=['kcn', 'ident'], writes=['pb0'])
                                    E('act', lambda e: e.activation(out=kcT_dst[0:64, g, ct * 128:ct * 128 + cn_], in_=pb[0][0:64, 0:cn_], func=AF.Copy),
                                      reads=['pb0'], writes=[kres])
                                else:
                                    E('act', lambda e: e.activation(out=vc_dst(ct)[0:cn_, g, :], in_=pf[0][0:cn_, 0:64], func=AF.Copy),
                                      reads=['pf0'], writes=[vres])
                return compress

            sbc = sb_t(nst, "sbc", [128, 4 * NCMP], F32)
            pbf = sb_t(nst, "pbf", [128, 4 * NCMP], BF16)
            mx = sb_t(nst, "mx", [128, 16], F32)
            imp = sb_t(nst, "imp", [128, 2, NBLK], F32)
            sc = sb_t(nst, "sc", [128, 2, NBLK], F32)
            scw = sb_t(nst, "scw", [128, NCMP], F32)
            m8 = sb_t(nst, "m8", [128, 16], F32)
            negq = sb_t(nst, "negq", [128, 2, NBLK], BF16)
            acc = sb_t(nst, "acc", [128, 8, 64], F32)
            fac = sb_t(nst, "fac", [128, 8], F32)
            tmpo = sb_t(nst, "tmpo", [128, 4, 64], F32)
            pTs = sb_t(nst, "pTs", [128, 4 * ((NCMP + 127) // 128) * 128], BF16)
            Pex = [sb_t(nst, f"Pex{i}", [128, 512], BF16) for i in range(2)]
            NEGT = sb_t(nst, "NEGT", [128, 2, 128], BF16)

            def nsa_compressed(R, ncmp, nblk, kth, q_fn, kcT, vc_fn, bias_ap, mul_ap, add_ap, gate_ap, gres):
                nct = (ncmp + 127) // 128
                for g in range(2):
                    Sc = [pf[2 + (r // 2)] for r in range(4)] if ncmp > 128 else [pf[2]] * 4
                    Scn = ['pf2', 'pf3'] if ncmp > 128 else ['pf2']
                    for r in range(4):
                        h = 4 * g + r
                        off = (r % 2) * ncmp if ncmp > 128 else r * ncmp
                        E('pe', lambda e: e.matmul(Sc[r][0:R, off:off + ncmp], q_fn(h), kcT[0:64, g, 0:ncmp], start=True, stop=True),
                          reads=['QTq', 'kcT'], writes=Scn)
                    sv = sbc[0:R, 0:4 * ncmp]
                    for r in range(4):
                        h = 4 * g + r
                        off = (r % 2) * ncmp if ncmp > 128 else r * ncmp
                        E('dve', lambda e: e.scalar_tensor_tensor(out=sbc[0:R, r * ncmp:(r + 1) * ncmp], in0=Sc[r][0:R, off:off + ncmp], scalar=0.125,
                                                                  in1=bias_ap[0:R, h * ncmp:(h + 1) * ncmp], op0=ALU.mult, op1=ALU.add),
                          reads=Scn + ['biasc'], writes=['sbc'])
                    s3 = sv.rearrange("p (r c) -> p r c", r=4)
                    E('dve', lambda e: e.tensor_reduce(out=mx[0:R, 0:4], in_=s3, axis=AX.X, op=ALU.max), reads=['sbc'], writes=['mx'])
                    E('dve', lambda e: e.tensor_scalar(mx[0:R, 0:4], mx[0:R, 0:4], -1e4, None, op0=ALU.max), reads=['mx'], writes=['mx'])
                    E('dve', lambda e: e.tensor_tensor(out=s3, in0=s3, in1=mx[0:R, 0:4].unsqueeze(2).broadcast_to([R, 4, ncmp]), op=ALU.subtract),
                      reads=['sbc', 'mx'], writes=['sbc'])
                    E('act', lambda e: e.activation(out=sv, in_=sv, func=AF.Exp), reads=['sbc'], writes=['sbc'])
                    E('dve', lambda e: e.tensor_reduce(out=mx[0:R, 4:8], in_=s3, axis=AX.X, op=ALU.add), reads=['sbc'], writes=['mx'])
                    E('dve', lambda e: e.tensor_scalar(mx[0:R, 4:8], mx[0:R, 4:8], 1e-30, None, op0=ALU.max), reads=['mx'], writes=['mx'])
                    E('dve', lambda e: e.reciprocal(mx[0:R, 8:12], mx[0:R, 4:8]), reads=['mx'], writes=['mx'])
                    E('dve', lambda e: e.tensor_tensor(out=s3, in0=s3, in1=mx[0:R, 8:12].unsqueeze(2).broadcast_to([R, 4, ncmp]), op=ALU.mult),
                      reads=['sbc', 'mx'], writes=['sbc'])
                    E('act', lambda e: e.activation(out=pbf[0:R, 0:4 * ncmp], in_=sv, func=AF.Copy), reads=['sbc'], writes=['pbf'])
                    E('dve', lambda e: e.tensor_reduce(out=scw[0:R, 0:ncmp], in_=sv.rearrange("p (r c) -> p c r", r=4), axis=AX.X, op=ALU.add),
                      reads=['sbc'], writes=['scw'])
                    E('dve', lambda e: e.tensor_reduce(out=imp[0:R, g, 0:nblk], in_=scw[0:R, 0:ncmp].rearrange("p (b j) -> p b j", j=2),
                                                       axis=AX.X, op=ALU.add), reads=['scw'], writes=['imp'])
                    E('dve', lambda e: e.tensor_tensor(out=sc[0:R, g, 0:nblk], in0=imp[0:R, g, 0:nblk], in1=mul_ap, op=ALU.mult),
                      reads=['imp', 'selm'], writes=['sc'])
                    E('dve', lambda e: e.tensor_tensor(out=sc[0:R, g, 0:nblk], in0=sc[0:R, g, 0:nblk], in1=add_ap, op=ALU.add),
                      reads=['sc', 'selm'], writes=['sc'])
                    E('dve', lambda e: e.max(out=m8[0:R, 0:8], in_=sc[0:R, g, 0:nblk]), reads=['sc'], writes=['m8'])
                    E('dve', lambda e: e.match_replace(out=scw[0:R, 0:nblk], in_to_replace=m8[0:R, 0:8], in_values=sc[0:R, g, 0:nblk], imm_value=-1e9),
                      reads=['sc', 'm8'], writes=['scw'])
                    E('dve', lambda e: e.max(out=m8[0:R, 8:16], in_=scw[0:R, 0:nblk]), reads=['scw'], writes=['m8'])
                    E('dve', lambda e: e.tensor_scalar(scw[0:R, 0:nblk], sc[0:R, g, 0:nblk], m8[0:R, 8 + kth:9 + kth], None, op0=ALU.is_ge),
                      reads=['sc', 'm8'], writes=['scw'])
                    E('dve', lambda e: e.tensor_scalar(negq[0:R, g, 0:nblk], scw[0:R, 0:nblk], -1.0, -NEG, op0=ALU.add, op1=ALU.mult),
                      reads=['scw'], writes=['negq'])
                    E('pe', lambda e: e.transpose(pb[0][0:nblk, g * 128:g * 128 + R], negq[0:R, g, 0:nblk], ident[0:R, 0:R]),
                      reads=['negq', 'ident'], writes=['pb0'])
                    pTv = pb[1][:, 0:4 * nct * 128].rearrange("p (r t q) -> p r t q", r=4, t=nct)
                    for r in range(4):
                        for ct in range(nct):
                            cn_ = min(128, ncmp - ct * 128)
                            E('pe', lambda e: e.transpose(pTv[0:cn_, r, ct, 0:R], pbf[0:R, r * ncmp + ct * 128:r * ncmp + ct * 128 + cn_],
                                                          ident[0:R, 0:R]), reads=['pbf', 'ident'], writes=['pb1'])
                    cpn = min(128, ncmp)
                    pTsv = pTs[:, 0:4 * nct * 128].rearrange("p (r t q) -> p r t q", r=4, t=nct)
                    E('act', lambda e: e.activation(out=pTsv[0:cpn, :, :, 0:R], in_=pTv[0:cpn, :, :, 0:R], func=AF.Copy), reads=['pb1'], writes=['pTs'])
                    Oc = pf[4][:, 0:256].rearrange("p (r d) -> p r d", r=4)
                    for r in range(4):
                        for ct in range(nct):
                            cn_ = min(128, ncmp - ct * 128)
                            E('pe', lambda e: e.matmul(Oc[0:R, r, :], pTsv[0:cn_, r, ct, 0:R], vc_fn(ct)[0:cn_, g, :],
                                                       start=(ct == 0), stop=(ct == nct - 1), skip_group_check=True),
                              reads=['pTs', 'vc'], writes=['pf4'])
                    E('dve', lambda e: e.tensor_tensor(out=acc[0:R, 4 * g:4 * g + 4, :], in0=Oc[0:R, :, :],
                                                       in1=gate_ap[:, 4 * g:4 * g + 4, 0:1].broadcast_to([R, 4, 64]), op=ALU.mult),
                      reads=['pf4', gres], writes=['acc'])
                E('act', lambda e: e.activation(out=NEGT[0:nblk, :, 0:R], in_=pb[0][0:nblk, 0:256].rearrange("p (g q) -> p g q", g=2)[:, :, 0:R],
                                                func=AF.Copy), reads=['pb0'], writes=['NEGT'])

            def branch_finish(R, g, Os, Osn, gate_ap, gres, bi):
                E('dve', lambda e: e.reciprocal(fac[0:R, 0:4], Os[0:R, :, 64]), reads=[Osn], writes=['fac'])
                E('dve', lambda e: e.tensor_tensor(out=fac[0:R, 4:8], in0=fac[0:R, 0:4], in1=gate_ap[:, 4 * g:4 * g + 4, bi], op=ALU.mult),
                  reads=['fac', gres], writes=['fac'])
                E('dve', lambda e: e.tensor_tensor(out=tmpo[0:R, :, :], in0=Os[0:R, :, 0:64],
                                                   in1=fac[0:R, 4:8].unsqueeze(2).broadcast_to([R, 4, 64]), op=ALU.mult),
                  reads=[Osn, 'fac'], writes=['tmpo'])
                E('dve', lambda e: e.tensor_tensor(out=acc[0:R, 4 * g:4 * g + 4, :], in0=acc[0:R, 4 * g:4 * g + 4, :], in1=tmpo[0:R, :, :], op=ALU.add),
                  reads=['acc', 'tmpo'], writes=['acc'])

            with (contextlib.ExitStack() if which == 'prompt' else contextlib.nullcontext()) as st:
              if which == 'prompt':
                  kcT = sb_t(st, "kcT", [64, 2, 64], BF16)
                  vc = sb_t(st, "vc", [64, 2, 64], BF16)
                  esel = sb_t(st, "esel", [32, T], BF16)
                  biasc = [sb_t(st, f"biasc{i}", [128, 512], F32) for i in range(2)]
                  selm = [sb_t(st, f"selm{i}", [128, 2, 32], F32) for i in range(2)]
                  load_const(esel[:], cd["esel_p"], 'esel', cast=True)
                  S.barrier()
                  with contextlib.ExitStack() as cs:
                      compress = make_compress(cs)
                      if 'cmpp' not in skip:
                          compress(lambda kind, g, l: XT[64 * g:64 * g + 64, kind, l:T:32], 64, kcT, lambda ct: vc, 'kcT', 'vc')
                      S.barrier()
                  step = 0
                  for qt in (range(NT) if 'nsap' not in skip else ()):
                      bp = qt % 2
                      E('sp', lambda e: e.dma_start(out=biasc[bp][:], in_=cd["biasc_p"][qt]), writes=['biasc'], dma=f'biasc{bp}')
                      E('sp', lambda e: e.dma_start(out=selm[bp][:, 0, :], in_=cd["mulm_p"][qt]), writes=['selm'], dma=f'selm{bp}')
                      E('sp', lambda e: e.dma_start(out=selm[bp][:, 1, :], in_=cd["addm_p"][qt]), writes=['selm'], dma=f'selm{bp}')
                      gate_ap = gates[:, qt, :].rearrange("p (h t) -> p h t", t=3)
                      nsa_compressed(128, 64, 32, 7, lambda h: QTb[0:64, qt, h, :], kcT, lambda ct: vc, biasc[bp], selm[bp][:, 0, :], selm[bp][:, 1, :],
                                     gate_ap, 'gates')
                      for g in range(2):
                          for (bi, KT, Vt, kts) in ((1, KTsel, Vsel, list(range(0, qt + 1))), (2, KTwin, Vwin, list(range(max(0, qt - 4), qt + 1)))):
                              Os = pf[4 + g][:, 0:260].rearrange("p (r c) -> p r c", r=4)
                              Osn = f'pf{4 + g}'
                              E('pe', lambda e: e.matmul(pf[4 + g][:, 0:260], zerom[:, :], zrhs[:, 0:260], start=True, stop=True, skip_group_check=True),
                                reads=['zerom', 'zrhs'], writes=[Osn])
                              for ki, kt in enumerate(kts):
                                  par = step % 2
                                  step += 1
                                  A, An = pf[par], f'pf{par}'
                                  Av = A[:, :].rearrange("p (r q) -> p r q", r=4)
                                  last = (bi == 2 and kt != qt and kt != qt - 4)
                                  E('pe', lambda e: e.matmul(Av, KT[0:68, g, kt * 128:(kt + 1) * 128], QTb[0:68, qt, 4 * g:4 * g + 4, :],
                                                             start=True, stop=last), reads=['KTx', 'QTb'], writes=[An])
                                  if bi == 1:
                                      E('pe', lambda e: e.matmul(Av, esel[0:32, kt * 128:(kt + 1) * 128],
                                                                 NEGT[0:32, g, :].unsqueeze(1).broadcast_to([32, 4, 128]),
                                                                 start=False, stop=(kt != qt)), reads=['esel', 'NEGT'], writes=[An])
                                  if kt == qt:
                                      E('pe', lambda e: e.matmul(Av, ident[:, :], cneg[:, :].unsqueeze(1).broadcast_to([128, 4, 128]),
                                                                 start=False, stop=True), reads=['ident', 'cneg'], writes=[An])
                                  elif bi == 2 and kt == qt - 4:
                                      E('pe', lambda e: e.matmul(Av, ident[:, :], bneg[:, :].unsqueeze(1).broadcast_to([128, 4, 128]),
                                                                 start=False, stop=True), reads=['ident', 'bneg'], writes=[An])
                                  E('act', lambda e: e.activation(out=Pex[par][:, :], in_=A[:, :], func=AF.Exp, scale=0.125),
                                    reads=[An], writes=[f'Pex{par}'])
                                  for r in range(4):
                                      E('pe', lambda e: e.matmul(Os[:, r, :], Pex[par][:, r * 128:(r + 1) * 128], Vt[:, kt, g, :],
                                                                 start=False, stop=True, skip_group_check=True),
                                        reads=[f'Pex{par}', 'Vx'], writes=[Osn])
                              branch_finish(128, g, Os, Osn, gate_ap, 'gates', bi)
                      E('dve', lambda e: e.tensor_tensor(out=ub[:, qt, :], in0=acc[:, :, :].rearrange("p h d -> p (h d)"), in1=gbp[:, qt, :], op=ALU.mult),
                        reads=['acc', 'gb'], writes=['ub'])
            S.barrier()

            with (contextlib.ExitStack() if which == 'sample' else contextlib.nullcontext()) as st:
              if which == 'sample':
                  XTs = sb_t(st, "XTs", [128, 2, PAST], BF16)
                  kcTs = sb_t(st, "kcTs", [64, 2, 256], BF16)
                  vcs = sb_t(st, "vcs", [128, 2, 2, 64], BF16)
                  esel_s = sb_t(st, "esel_s", [128, PAST], BF16)
                  aks = sb_t(st, "aks", [68, PAST], BF16)
                  biascs = sb_t(st, "biascs", [SQ, 8 * 256], F32)
                  selms = sb_t(st, "selms", [SQ, 2, 128], F32)
                  pg2 = [sb_t(st, f"pg2{i}", [128, 512], BF16) for i in range(2)]
                  vpg = [sb_t(st, f"vpg{i}", [128, 2, 65], BF16) for i in range(2)]
                  KTpg = [sb_t(st, f"KTpg{i}", [68, 2, 128], BF16) for i in range(2)]
                  swt = sb_t(st, "swt", [128, 4, 256], BF16)
                  vwt = sb_t(st, "vwt", [128, 4, 2, 65], BF16)
                  KTws = sb_t(st, "KTws", [68, 2, 512], BF16)
                  idxf = sb_t(st, "idxf2", [128, NPAGE], F32)
                  idxi = sb_t(st, "idxi2", [128, NPAGE], I32)
                  ptb = sb_t(st, "ptb2", [128, NPAGE], I32)
                  load_const(esel_s[:], cd["esel_s"], 'esel_s', cast=True)
                  load_const(aks[64:68, :], cd["ak"][:, 0:PAST], 'aks', cast=True)
                  load_const(biascs[:], cd["biasc_s"], 'biasc')
                  load_const(selms[:, 0, :], cd["mulm_s"], 'selm'); load_const(selms[:, 1, :], cd["addm_s"], 'selm')
                  for g in range(2):
                      load_const(KTws[64:68, g, :], cd["ak"][:, PAST - 512:PAST], 'KTws', cast=True)
                  E('dve', lambda e: e.memset(vpg[0][:, :, 64:65], 1.0), writes=['vpg0'])
                  E('dve', lambda e: e.memset(vpg[1][:, :, 64:65], 1.0), writes=['vpg1'])
                  E('dve', lambda e: e.memset(vwt[:, :, :, 64:65], 1.0), writes=['vwt'])
                  S.barrier()
                  compress = make_compress(st)
                  step = 0
                  for b in (range(SB) if 'nsas' not in skip else ()):
                      E('sp', lambda e: e.dma_start(out=ptb[:], in_=ptab[b:b + 1, :].partition_broadcast(128)), writes=['ptb2'], dma='ptb2')
                      E('dve', lambda e: e.tensor_copy(idxf[:], ptb[:]), reads=['ptb2'], writes=['idxf2'])
                      E('dve', lambda e: e.tensor_scalar(idxf[:], idxf[:], 128.0, iotap[:, 0:1], op0=ALU.mult, op1=ALU.add),
                        reads=['idxf2', 'iotap'], writes=['idxf2'])
                      E('dve', lambda e: e.tensor_copy(idxi[:], idxf[:]), reads=['idxf2'], writes=['idxi2'])
                      for pg in range(NPAGE):
                          par = pg % 2
                          for (cap, clo, cw) in [c_ for c_ in nsa_chunks if c_[1] < 256]:
                              E('pool', lambda e: e.indirect_dma_start(out=pg2[par][:, clo:clo + cw], out_offset=None, in_=cap,
                                                                       in_offset=bass.IndirectOffsetOnAxis(ap=idxi[:, pg:pg + 1], axis=0)),
                                reads=['idxi2', 'cache_nsa'], writes=[f'pg2{par}'], dma=f'pg2{par}')
                          pT = pb[par][:, :].rearrange("p (k c) -> p k c", k=8)
                          for kind in range(2):
                              E('pe', lambda e: e.transpose(pT[:, kind, :], pg2[par][:, kind * 128:(kind + 1) * 128], ident[:, :]),
                                reads=[f'pg2{par}', 'ident'], writes=[f'pb{par}'])
                          E('dve', lambda e: e.tensor_copy(XTs[:, :, pg * 128:(pg + 1) * 128], pT[:, 0:2, :]), reads=[f'pb{par}'], writes=['XTsrc'])
                      compress(lambda kind, g, l: XTs[64 * g:64 * g + 64, kind, l:PAST:32], 256, kcTs, lambda ct: vcs[:, ct, :, :], 'kcT', 'vc')
                      gate_ap = gates_s[:, b, :].rearrange("p (h t) -> p h t", t=3)
                      nsa_compressed(SQ, 256, 128, 6, lambda h: QTb_s[0:64, b, h, :], kcTs, lambda ct: vcs[:, ct, :, :], biascs, selms[:, 0, :], selms[:, 1, :],
                                     gate_ap, 'gates_s')
                      Osg = [pf[4][:, 0:260].rearrange("p (r c) -> p r c", r=4), pf[5][:, 0:260].rearrange("p (r c) -> p r c", r=4)]
                      par = step % 2
                      step += 1
                      A, An = pf[par], f'pf{par}'
                      for g in range(2):
                          Av = A[:, g * 32:(g + 1) * 32].rearrange("p (r q) -> p r q", r=4)
                          E('pe', lambda e: e.matmul(Av[0:8], KTsel_s[0:68, b, g, :], QTb_s[0:68, b, 4 * g:4 * g + 4, :], start=True, stop=False),
                            reads=['KTsel_s', 'QTb_s'], writes=[An])
                          E('pe', lambda e: e.matmul(Av[0:8], ident[0:8, 0:8], cneg8[0:8, :].unsqueeze(1).broadcast_to([8, 4, 8]), start=False, stop=True),
                            reads=['ident', 'cneg8'], writes=[An])
                      E('act', lambda e: e.activation(out=Pex[par][0:8, 0:64], in_=A[0:8, 0:64], func=AF.Exp, scale=0.125), reads=[An], writes=[f'Pex{par}'])
                      for g in range(2):
                          E('pe', lambda e: e.matmul(pf[4 + g][0:8, 0:260], zerom[:, 0:8], zrhs[:, 0:260], start=True, stop=True, skip_group_check=True),
                            reads=['zerom', 'zrhs'], writes=[f'pf{4 + g}'])
                          for r in range(4):
                              E('pe', lambda e: e.matmul(Osg[g][0:8, r, :], Pex[par][0:8, g * 32 + r * 8:g * 32 + r * 8 + 8], Vsel_s[0:8, b, g, :],
                                                         start=False, stop=True, skip_group_check=True),
                                reads=[f'Pex{par}', 'Vsel_s'], writes=[f'pf{4 + g}'])
                      for pg in range(NPAGE):
                          par = step % 2
                          step += 1
                          A, An = pf[par], f'pf{par}'
                          for (cap, clo, cw) in [c_ for c_ in nsa_chunks if c_[1] + c_[2] > 256]:
                              E('pool', lambda e: e.indirect_dma_start(out=pg2[par][:, clo:clo + cw], out_offset=None, in_=cap,
                                                                       in_offset=bass.IndirectOffsetOnAxis(ap=idxi[:, pg:pg + 1], axis=0)),
                                reads=['idxi2', 'cache_nsa'], writes=[f'pg2{par}'], dma=f'pg2{par}')
                          pT = pb[par][:, :].rearrange("p (k c) -> p k c", k=8)
                          for g in range(2):
                              E('pe', lambda e: e.transpose(pT[0:64, g, :], pg2[par][:, 256 + g * 64:256 + (g + 1) * 64], ident[:, :]),
                                reads=[f'pg2{par}', 'ident'], writes=[f'pb{par}'])
                          E('dve', lambda e: e.tensor_copy(KTpg[par][0:64, :, :], pT[0:64, 0:2, :]), reads=[f'pb{par}'], writes=[f'KTpg{par}'])
                          E('pool', lambda e: e.tensor_copy(KTpg[par][64:68, :, :],
                                                            aks[64:68, pg * 128:(pg + 1) * 128].unsqueeze(1).broadcast_to([4, 2, 128])),
                            reads=['aks'], writes=[f'KTpg{par}'])
                          E('dve', lambda e: e.tensor_copy(vpg[par][:, :, 0:64], pg2[par][:, 384:512].rearrange("p (g d) -> p g d", g=2)),
                            reads=[f'pg2{par}'], writes=[f'vpg{par}'])
                          for g in range(2):
                              Av = A[:, g * 32:(g + 1) * 32].rearrange("p (r q) -> p r q", r=4)
                              E('pe', lambda e: e.matmul(Av, KTpg[par][0:68, g, :], QTb_s[0:68, b, 4 * g:4 * g + 4, :], start=True, stop=False),
                                reads=[f'KTpg{par}', 'QTb_s'], writes=[An])
                              E('pe', lambda e: e.matmul(Av, esel_s[:, pg * 128:(pg + 1) * 128],
                                                         NEGT[:, g, 0:8].unsqueeze(1).broadcast_to([128, 4, 8]), start=False, stop=True),
                                reads=['esel_s', 'NEGT'], writes=[An])
                          E('act', lambda e: e.activation(out=Pex[par][:, 0:64], in_=A[:, 0:64], func=AF.Exp, scale=0.125), reads=[An], writes=[f'Pex{par}'])
                          for g in range(2):
                              for r in range(4):
                                  E('pe', lambda e: e.matmul(Osg[g][0:8, r, :], Pex[par][:, g * 32 + r * 8:g * 32 + r * 8 + 8], vpg[par][:, g, :],
                                                             start=False, stop=True, skip_group_check=True),
                                    reads=[f'Pex{par}', f'vpg{par}'], writes=[f'pf{4 + g}'])
                      for g in range(2):
                          branch_finish(SQ, g, Osg[g], f'pf{4 + g}', gate_ap, 'gates_s', 1)
                      E('pool', lambda e: e.dma_start(out=swt[:], in_=state_win[b].rearrange("(t p) c -> p t c", p=128)), writes=['swt'], dma='swt')
                      for t4 in range(4):
                          pT = pb[t4 % 2][:, :].rearrange("p (k c) -> p k c", k=8)
                          for g in range(2):
                              E('pe', lambda e: e.transpose(pT[0:64, g, :], swt[:, t4, g * 64:(g + 1) * 64], ident[:, :]),
                                reads=['swt', 'ident'], writes=[f'pb{t4 % 2}'])
                          E('dve', lambda e: e.tensor_copy(KTws[0:64, :, t4 * 128:(t4 + 1) * 128], pT[0:64, 0:2, :]), reads=[f'pb{t4 % 2}'], writes=['KTws'])
                      E('dve', lambda e: e.tensor_copy(vwt[:, :, :, 0:64], swt[:, :, 128:256].rearrange("p t (g d) -> p t g d", g=2)),
                        reads=['swt'], writes=['vwt'])
                      for g in range(2):
                          E('pe', lambda e: e.matmul(pf[4 + g][0:8, 0:260], zerom[:, 0:8], zrhs[:, 0:260], start=True, stop=True, skip_group_check=True),
                            reads=['zerom', 'zrhs'], writes=[f'pf{4 + g}'])
                      for t4 in range(5):
                          par = step % 2
                          step += 1
                          A, An = pf[par], f'pf{par}'
                          K_ = 128 if t4 < 4 else 8
                          for g in range(2):
                              Av = A[:, g * 32:(g + 1) * 32].rearrange("p (r q) -> p r q", r=4)
                              if t4 < 4:
                                  E('pe', lambda e: e.matmul(Av, KTws[0:68, g, t4 * 128:(t4 + 1) * 128], QTb_s[0:68, b, 4 * g:4 * g + 4, :],
                                                             start=True, stop=(t4 != 0)), reads=['KTws', 'QTb_s'], writes=[An])
                                  if t4 == 0:
                                      E('pe', lambda e: e.matmul(Av, ident[:, :], bneg8[:, :].unsqueeze(1).broadcast_to([128, 4, 8]), start=False, stop=True),
                                        reads=['ident', 'bneg8'], writes=[An])
                              else:
                                  E('pe', lambda e: e.matmul(Av[0:8], KTwin_s[0:68, b, g, :], QTb_s[0:68, b, 4 * g:4 * g + 4, :], start=True, stop=False),
                                    reads=['KTwin_s', 'QTb_s'], writes=[An])
                                  E('pe', lambda e: e.matmul(Av[0:8], ident[0:8, 0:8], cneg8[0:8, :].unsqueeze(1).broadcast_to([8, 4, 8]), start=False, stop=True),
                                    reads=['ident', 'cneg8'], writes=[An])
                          E('act', lambda e: e.activation(out=Pex[par][0:K_, 0:64], in_=A[0:K_, 0:64], func=AF.Exp, scale=0.125), reads=[An], writes=[f'Pex{par}'])
                          for g in range(2):
                              for r in range(4):
                                  vsrc = vwt[:, t4, g, :] if t4 < 4 else Vwin_s[0:8, b, g, :]
                                  E('pe', lambda e: e.matmul(Osg[g][0:8, r, :], Pex[par][0:K_, g * 32 + r * 8:g * 32 + r * 8 + 8], vsrc,
                                                             start=False, stop=True, skip_group_check=True),
                                    reads=[f'Pex{par}', 'vwt', 'Vwin_s'], writes=[f'pf{4 + g}'])
                      for g in range(2):
                          branch_finish(SQ, g, Osg[g], f'pf{4 + g}', gate_ap, 'gates_s', 2)
                      E('dve', lambda e: e.tensor_tensor(out=ub_s[:, b, :], in0=acc[0:SQ, :, :].rearrange("p h d -> p (h d)"), in1=gb_s[:, b, :], op=ALU.mult),
                        reads=['acc', 'gb_s'], writes=['ub_s'])
            S.barrier()
        nsa_phase('prompt')
        S.barrier()
        nsast.close()
        S.barrier()
        nsa_phase('sample')
        S.barrier()

        with contextlib.ExitStack() as st:
            wout = sb_t(st, "wout", [128, 8, D], BF16)
            wpg = sb_t(st, "wpg", [128, 8, D], BF16)
            wple = sb_t(st, "wple", [128, 2, D], BF16)
            load_const(wout[:], wout_d, 'wout', cast=True)
            load_const(wpg[:], wpg_d, 'wpg', cast=True)
            load_const(wple[:], wple_d, 'wple', cast=True)
            S.barrier()
            xt = [sb_t(st, f"oxt{i}", [128, D], F32) for i in range(2)]
            pt_ = [sb_t(st, f"opt{i}", [128, 256], F32) for i in range(2)]
            ptb_ = sb_t(st, "optb", [128, 256], BF16)
            uT = sb_t(st, "uT", [128, 8, 128], BF16)
            h32 = sb_t(st, "h32", [128, D], F32)
            hbf = sb_t(st, "hbf", [128, D], BF16)
            hT2 = sb_t(st, "hT2", [128, 8, 128], BF16)
            pT2 = sb_t(st, "pT2", [128, 2, 128], BF16)
            sg = sb_t(st, "sg", [128, D], F32)
            yo = [sb_t(st, f"yo{i}", [128, D], F32) for i in range(2)]
            if debug:
                ud = sb_t(st, "ud", [128, D], F32)
            for ti in (range(NT + SB) if 'out' not in skip else ()):
                is_p = ti < NT
                R = 128 if is_p else SQ
                b = ti - NT
                par = ti % 2
                r0 = ti * 128 if is_p else b * SQ
                ua_ap = ua[:, ti, :] if is_p else ua_s[:, b, :]
                ub_ap = ub[:, ti, :] if is_p else ub_s[:, b, :]
                xsrc = (x_p if is_p else x_s)[r0:r0 + R, :]
                psrc = (p_p if is_p else p_s)[r0:r0 + R, :]
                ydst = (y_p if is_p else y_s)[r0:r0 + R, :]
                E('sp', lambda e: e.dma_start(out=xt[par][0:R, :], in_=xsrc), writes=[f'oxt{par}'], dma=f'oxt{par}')
                E('sp', lambda e: e.dma_start(out=pt_[par][0:R, :], in_=psrc), writes=[f'opt{par}'], dma=f'opt{par}')
                if debug:
                    E('dve', lambda e: e.tensor_copy(ud[0:R, 0:512], ua_ap), reads=['ua', 'ua_s'], writes=['ud'])
                    E('dve', lambda e: e.tensor_copy(ud[0:R, 512:1024], ub_ap), reads=['ub', 'ub_s'], writes=['ud'])
                    E('pool', lambda e: e.dma_start(out=(u_pd if is_p else u_sd)[r0:r0 + R, :], in_=ud[0:R, :]), reads=['ud'], dma='o_ud')
                pT = pb[0][:, :].rearrange("p (k c) -> p k c", k=8)
                for k in range(8):
                    src = ua_ap[:, k * 128:(k + 1) * 128] if k < 4 else ub_ap[:, (k - 4) * 128:(k - 3) * 128]
                    E('pe', lambda e: e.transpose(pT[:, k, 0:R], src, ident[0:R, 0:R]), reads=['ua', 'ub', 'ua_s', 'ub_s', 'ident'], writes=['pb0'])
                E('act', lambda e: e.activation(out=uT[:, :, 0:R], in_=pT[:, :, 0:R], func=AF.Copy), reads=['pb0'], writes=['uT'])
                for half in range(2):
                    for k in range(8):
                        E('pe', lambda e: e.matmul(pf[half][0:R, :], uT[:, k, 0:R], wout[:, k, half * 512:(half + 1) * 512], start=(k == 0), stop=(k == 7)),
                          reads=['uT', 'wout'], writes=[f'pf{half}'])
                    E('dve', lambda e: e.tensor_tensor(out=h32[0:R, half * 512:(half + 1) * 512], in0=pf[half][0:R, :],
                                                       in1=xt[par][0:R, half * 512:(half + 1) * 512], op=ALU.add),
                      reads=[f'pf{half}', f'oxt{par}'], writes=['h32'])
                E('act', lambda e: e.activation(out=hbf[0:R, :], in_=h32[0:R, :], func=AF.Copy), reads=['h32'], writes=['hbf'])
                pTb = pb[1][:, :].rearrange("p (k c) -> p k c", k=8)
                for k in range(8):
                    E('pe', lambda e: e.transpose(pTb[:, k, 0:R], hbf[0:R, k * 128:(k + 1) * 128], ident[0:R, 0:R]), reads=['hbf', 'ident'], writes=['pb1'])
                E('act', lambda e: e.activation(out=hT2[:, :, 0:R], in_=pTb[:, :, 0:R], func=AF.Copy), reads=['pb1'], writes=['hT2'])
                E('dve', lambda e: e.tensor_copy(ptb_[0:R, :], pt_[par][0:R, :]), reads=[f'opt{par}'], writes=['optb'])
                for k in range(2):
                    E('pe', lambda e: e.transpose(pT[:, k, 0:R], ptb_[0:R, k * 128:(k + 1) * 128], ident[0:R, 0:R]), reads=['optb', 'ident'], writes=['pb0'])
                E('act', lambda e: e.activation(out=pT2[:, :, 0:R], in_=pT[:, 0:2, 0:R], func=AF.Copy), reads=['pb0'], writes=['pT2'])
                for half in range(2):
                    for k in range(8):
                        E('pe', lambda e: e.matmul(pf[2 + half][0:R, :], hT2[:, k, 0:R], wpg[:, k, half * 512:(half + 1) * 512], start=(k == 0), stop=(k == 7)),
                          reads=['hT2', 'wpg'], writes=[f'pf{2 + half}'])
                    for k in range(2):
                        E('pe', lambda e: e.matmul(pf[4 + half][0:R, :], pT2[:, k, 0:R], wple[:, k, half * 512:(half + 1) * 512], start=(k == 0), stop=(k == 1)),
                          reads=['pT2', 'wple'], writes=[f'pf{4 + half}'])
                    hs = slice(half * 512, (half + 1) * 512)
                    sigmoid_from(pf[2 + half][0:R, :], sg[0:R, hs], R, [f'pf{2 + half}'], 'sg')
                    E('dve', lambda e: e.tensor_tensor(out=sg[0:R, hs], in0=sg[0:R, hs], in1=pf[4 + half][0:R, :], op=ALU.mult),
                      reads=['sg', f'pf{4 + half}'], writes=['sg'])
                    E('dve', lambda e: e.tensor_tensor(out=yo[par][0:R, hs], in0=sg[0:R, hs], in1=h32[0:R, hs], op=ALU.add),
                      reads=['sg', 'h32'], writes=[f'yo{par}'])
                E('pool', lambda e: e.dma_start(out=ydst, in_=yo[par][0:R, :]), reads=[f'yo{par}'], dma=f'o_y{par}')
            for b in range(SB):
                E('pool', lambda e: e.dma_start(out=win_s[b, 0:504, :], in_=state_win[b, 8:512, :]), dma='o_winold')
            S.barrier()
        S.barrier()
    return nc


_NC_CACHE = {}


def _prep_shared(norm_g, w_in, q_norm_g, kcmp_norm_g, ksel_norm_g, kwin_norm_g,
                 cmp_pos_k, cmp_w1_k, cmp_w2_k, cmp_pos_v, cmp_w1_v, cmp_w2_v, w_out, w_ple, w_ple_gate):
    f = lambda a: np.ascontiguousarray(np.asarray(a, dtype=np.float32))
    bc = lambda v: f(np.broadcast_to(np.asarray(v)[0][None, :], (128, np.asarray(v).shape[1])))
    w1 = lambda w: f(np.concatenate([np.asarray(w)[0].reshape(32, 64, 256).transpose(1, 0, 2)] * 2, axis=0))
    w2 = lambda w: f(np.asarray(w)[0].reshape(2, 128, 64).transpose(1, 0, 2))
    shared = dict(
        g_b=bc(norm_g),
        w_in=f(np.asarray(w_in)[0].reshape(8, 128, N_IN).transpose(1, 0, 2)),
        qg_b=bc(q_norm_g), kcg_b=bc(kcmp_norm_g), ksg_b=bc(ksel_norm_g), kwg_b=bc(kwin_norm_g),
        w1k=w1(cmp_w1_k), w1v=w1(cmp_w1_v), w2k=w2(cmp_w2_k), w2v=w2(cmp_w2_v),
        pekT=f(np.asarray(cmp_pos_k)[0].T), pevT=f(np.asarray(cmp_pos_v)[0].T),
        w_out=f(np.asarray(w_out)[0].reshape(8, 128, D).transpose(1, 0, 2)),
        w_pg=f(np.asarray(w_ple_gate)[0].reshape(8, 128, D).transpose(1, 0, 2)),
        w_ple=f(np.asarray(w_ple)[0].reshape(2, 128, D).transpose(1, 0, 2)),
    )
    for k, v in host_consts().items():
        shared["c_" + k] = f(v)
    return shared


def _core_inputs(c, xp, xs, pp, ps, sw, pt):
    f = lambda a: np.ascontiguousarray(np.asarray(a, dtype=np.float32))
    m = {}
    m["x_p"] = f(xp[c]); m["p_p"] = f(pp[0, c])
    m["x_s"] = f(xs[SB * c:SB * c + SB].reshape(SB * SQ, D))
    m["p_s"] = f(ps[0, SB * c:SB * c + SB].reshape(SB * SQ, 256))
    m["state_win"] = f(sw[0, SB * c:SB * c + SB].reshape(SB, 512, 256))
    m["ptab"] = np.ascontiguousarray(pt[SB * c:SB * c + SB].astype(np.int32))
    return m


def kernel(x_prompt, x_sample, p_prompt, p_sample, cache_sb, cache_nsa, state_win, page_table,
           norm_g, w_in, q_norm_g, kcmp_norm_g, ksel_norm_g, kwin_norm_g,
           cmp_pos_k, cmp_w1_k, cmp_w2_k, cmp_pos_v, cmp_w1_v, cmp_w2_v,
           w_out, w_ple, w_ple_gate):
    f = lambda a: np.ascontiguousarray(np.asarray(a, dtype=np.float32))
    if "nc" not in _NC_CACHE:
        _NC_CACHE["nc"] = build()
    nc = _NC_CACHE["nc"]
    shared = _prep_shared(norm_g, w_in, q_norm_g, kcmp_norm_g, ksel_norm_g, kwin_norm_g,
                          cmp_pos_k, cmp_w1_k, cmp_w2_k, cmp_pos_v, cmp_w1_v, cmp_w2_v, w_out, w_ple, w_ple_gate)
    csb = np.asarray(cache_sb).reshape(NPOOL * 128, 1024)
    cns = np.asarray(cache_nsa).reshape(NPOOL * 128, 512)
    sh = NPOOL // NCORE * 128
    xp = np.asarray(x_prompt); xs = np.asarray(x_sample)
    pp = np.asarray(p_prompt); ps = np.asarray(p_sample)
    sw = np.asarray(state_win); pt = np.asarray(page_table)
    in_maps = []
    for c in range(NCORE):
        m = dict(shared)
        m.update(_core_inputs(c, xp, xs, pp, ps, sw, pt))
        m["csb_sh"] = f(csb[c * sh:(c + 1) * sh].reshape(sh, 8, 128).transpose(1, 0, 2))
        m["cns_sh"] = f(cns[c * sh:(c + 1) * sh].reshape(sh, 4, 128).transpose(1, 0, 2))
        in_maps.append(m)
    res = run_bass_kernel_spmd(nc, in_maps, core_ids=list(range(NCORE)))
    R = res.results
    cat = lambda k: np.stack([np.asarray(R[c][k]) for c in range(NCORE)])
    y_p = cat("y_p").reshape(8, T, D)
    y_s = cat("y_s").reshape(32, SQ, D)
    sbp = cat("sb_p").reshape(1, 8, T, 2, 8, 64)
    sbs = cat("sb_s").reshape(1, 32, SQ, 2, 8, 64)
    nsp = cat("nsa_p").reshape(1, 8, T, 4, 2, 64)
    nss = cat("nsa_s").reshape(1, 32, SQ, 4, 2, 64)
    wp = cat("win_p").reshape(1, 8, 512, 2, 2, 64)
    ws = cat("win_s").reshape(1, 32, 512, 2, 2, 64)
    return (y_p, y_s, sbp, sbs, nsp, nss, wp, ws)
```

```python
import contextlib
import numpy as np
import concourse.bass as bass
import concourse.mybir as mybir
from concourse.bass_utils import run_bass_kernel_spmd

F32 = mybir.dt.float32
BF16 = mybir.dt.bfloat16
I32 = mybir.dt.int32
AF = mybir.ActivationFunctionType
ALU = mybir.AluOpType
AX = mybir.AxisListType

NCORE = 8
D = 1024
T = 2048
NT = 16
SB = 4
SQ = 8
NPAGE = 64
PAST = 8192
NPOOL = 2560
EPS = 1e-6
NEG = -30000.0
N_IN = 3864
SLOPES = [2.0 ** (-(h + 1)) for h in range(8)]

GROUPS = {"sbq": (0, 512), "sbk": (512, 512), "sbv": (1024, 512), "sbg": (1536, 512),
          "nq": (2048, 512), "ncs": (2560, 512), "nwb": (3072, 280), "ng": (3352, 512)}


import os
MAXE = int(os.environ.get('MAXE', '100000000'))


class Sched:
    def __init__(self, nc, stack):
        self.nc = nc
        self.stack = stack
        self.engs = {'pe': nc.tensor, 'act': nc.scalar, 'dve': nc.vector, 'pool': nc.gpsimd, 'sp': nc.sync}
        self.cnt = {}
        self.sems = {}
        self.seen = {e: {} for e in self.engs}
        self.lastw = {}
        self.rds = {}
        self.ninstr = {e: 0 for e in self.engs}

    def sem(self, name):
        if name not in self.sems:
            self.sems[name] = self.stack.enter_context(self.nc.semaphore(name))
            self.cnt[name] = 0
        return self.sems[name]

    def emit(self, engine, fn, reads=(), writes=(), dma=None, inc=None):
        self.total = getattr(self, 'total', 0) + 1
        if self.total > MAXE:
            return None
        if self.total == MAXE:
            print('LAST EMIT', engine, reads, writes, dma, flush=True)
        deps = set()
        for r in reads:
            if r in self.lastw:
                deps.add(self.lastw[r])
        for w in writes:
            if w in self.lastw:
                deps.add(self.lastw[w])
            for t in self.rds.get(w, ()):
                deps.add(t)
        own = 'E_' + engine
        waits = {}
        for (s, v) in deps:
            if s == own and engine == 'pe':
                continue
            if self.seen[engine].get(s, 0) >= v:
                continue
            waits[s] = max(waits.get(s, 0), v)
        eng = self.engs[engine]
        for s, v in waits.items():
            self.seen[engine][s] = v
            eng.wait_ge(self.sems[s], v)
        if dma is None:
            sname, inc = own, 1
        else:
            sname, inc = 'D_' + dma, (16 if inc is None else inc)
        sh = self.sem(sname)
        self.cnt[sname] += inc
        tok = (sname, self.cnt[sname])
        ins = fn(eng)
        ins.then_inc(sh, inc)
        self.ninstr[engine] += 1
        for w in writes:
            self.lastw[w] = tok
            self.rds[w] = []
        for r in reads:
            if r not in writes:
                self.rds.setdefault(r, []).append(tok)
        return tok

    def barrier(self):
        for e, eng in self.engs.items():
            for s, c in self.cnt.items():
                if c > 0 and self.seen[e].get(s, 0) < c and not (s == 'E_pe' and e == 'pe'):
                    eng.wait_ge(self.sems[s], c)
                    self.seen[e][s] = c
        self.lastw.clear()
        self.rds.clear()


def host_consts():
    c = {}
    p = np.arange(128)
    c["ident"] = np.eye(128, dtype=np.float32)
    c["tri"] = (p[:, None] >= p[None, :]).astype(np.float32)
    c["cotri"] = (p[:, None] < p[None, :]).astype(np.float32)
    c["onesm"] = np.ones((128, 128), np.float32)
    c["sneg"] = np.where(p[:, None] >= p[None, :], NEG, 0.0).astype(np.float32)
    c["cneg"] = np.where(p[:, None] > p[None, :], NEG, 0.0).astype(np.float32)
    c["bneg"] = np.where(p[:, None] <= p[None, :], NEG, 0.0).astype(np.float32)
    q8 = np.arange(8)
    c["sneg8"] = np.tile(np.where(p[:, None] >= q8[None, :], NEG, 0.0), (1, 8)).astype(np.float32)
    c["cneg8"] = np.where(p[:, None] > q8[None, :], NEG, 0.0).astype(np.float32)
    c["bneg8"] = np.where(p[:, None] <= q8[None, :], NEG, 0.0).astype(np.float32)
    sl = np.array(SLOPES, np.float64)
    t = np.arange(T)
    aq = np.zeros((4, NT, 8, 128), np.float64)
    tq = t.reshape(NT, 128)
    aq[0] = -8.0 * sl[None, :, None] * (256 * (tq // 256))[:, None, :]
    aq[1] = -8.0 * sl[None, :, None] * (tq % 256)[:, None, :]
    aq[2] = 8.0 * sl[None, :, None]
    aq[3] = 8.0 * sl[None, :, None]
    c["aq_p"] = aq.astype(np.float32)
    ts = PAST + q8
    aqs = np.zeros((4, 8, 8), np.float64)
    aqs[0] = -8.0 * sl[:, None] * (256 * (ts // 256))[None, :]
    aqs[1] = -8.0 * sl[:, None] * (ts % 256)[None, :]
    aqs[2] = 8.0 * sl[:, None]
    aqs[3] = 8.0 * sl[:, None]
    c["aq_s"] = np.tile(aqs[:, None, :, :], (1, SB, 1, 1)).astype(np.float32)
    kp = np.arange(PAST + 8)
    ak = np.stack([np.ones_like(kp), np.ones_like(kp), 256 * (kp // 256), kp % 256]).astype(np.float32)
    c["ak"] = ak
    cend = 32 * np.arange(64) + 31
    dist = t[:, None] - cend[None, :]
    b = -sl[None, :, None] * dist[:, None, :]
    b = np.where(dist[:, None, :] >= 0, b, -1e30)
    c["biasc_p"] = b.reshape(NT, 128, 8 * 64).astype(np.float32)
    cend_s = 32 * np.arange(256) + 31
    dist_s = ts[:, None] - cend_s[None, :]
    c["biasc_s"] = (-sl[None, :, None] * dist_s[:, None, :]).reshape(8, 8 * 256).astype(np.float32)
    blk = np.arange(32)
    cur = t // 64
    forced = (blk[None, :] == 0) | (blk[None, :] == cur[:, None]) | (blk[None, :] == cur[:, None] - 1)
    future = blk[None, :] > cur[:, None]
    c["mulm_p"] = (~(forced | future)).astype(np.float32).reshape(NT, 128, 32)
    c["addm_p"] = np.where(forced, 1e3, np.where(future, -1.0, 0.0)).astype(np.float32).reshape(NT, 128, 32)
    blk_s = np.arange(128)
    forced_s = (blk_s == 0) | (blk_s == 127)
    c["mulm_s"] = np.tile((~forced_s).astype(np.float32)[None, :], (8, 1))
    c["addm_s"] = np.tile(np.where(forced_s, 1e3, 0.0).astype(np.float32)[None, :], (8, 1))
    kk = np.arange(T)
    c["esel_p"] = (kk[None, :] // 64 == blk[:, None]).astype(np.float32)
    kk = np.arange(PAST)
    c["esel_s"] = (kk[None, :] // 64 == blk_s[:, None]).astype(np.float32)
    c["iota_p"] = p.astype(np.float32).reshape(128, 1)
    return c


CONST_SHAPES = None


def build(npool=NPOOL, allgather=True, debug=False, skip=()):
    nc = bass.Bass("TRN2", target_bir_lowering=False)
    consts = host_consts()

    def din(name, shape, dt=F32):
        return nc.dram_tensor(name, list(shape), dt, kind="ExternalInput").ap()

    def dout(name, shape, dt=F32):
        return nc.dram_tensor(name, list(shape), dt, kind="ExternalOutput").ap()

    x_p = din("x_p", [T, D]); p_p = din("p_p", [T, 256])
    x_s = din("x_s", [SB * SQ, D]); p_s = din("p_s", [SB * SQ, 256])
    state_win = din("state_win", [SB, 512, 256])
    ptab = din("ptab", [SB, NPAGE], I32)
    g_b = din("g_b", [128, D])
    w_in = din("w_in", [128, 8, N_IN])
    qg_b = din("qg_b", [128, 64]); kcg_b = din("kcg_b", [128, 64])
    ksg_b = din("ksg_b", [128, 64]); kwg_b = din("kwg_b", [128, 64])
    w1k_d = din("w1k", [128, 32, 256]); w1v_d = din("w1v", [128, 32, 256])
    w2k_d = din("w2k", [128, 2, 64]); w2v_d = din("w2v", [128, 2, 64])
    pek_d = din("pekT", [64, 32]); pev_d = din("pevT", [64, 32])
    wout_d = din("w_out", [128, 8, D]); wpg_d = din("w_pg", [128, 8, D]); wple_d = din("w_ple", [128, 2, D])
    cd = {k: din("c_" + k, v.shape) for k, v in consts.items()}
    if allgather:
        sh = npool // NCORE
        csb_sh = din("csb_sh", [8, sh * 128, 128]); cns_sh = din("cns_sh", [4, sh * 128, 128])
        ag_list = []
        sb_chunks = []; nsa_chunks = []
        for (nm, src, nchunk, lst) in (("csb", csb_sh, 8, sb_chunks), ("cns", cns_sh, 4, nsa_chunks)):
            for j in range(nchunk):
                loc_t = nc.dram_tensor(f"{nm}_loc{j}", [sh * 128, 128], F32)
                full_t = nc.dram_tensor(f"{nm}_full{j}", [npool * 128, 128], F32, addr_space="Shared")
                ag_list.append((src[j], loc_t, full_t))
                lst.append((full_t.ap(), 128 * j, 128))
    else:
        cache_sb = din("cache_sb", [npool * 128, 1024])
        cache_nsa = din("cache_nsa", [npool * 128, 512])
        sb_chunks = [(cache_sb, 0, 1024)]
        nsa_chunks = [(cache_nsa, 0, 512)]

    y_p = dout("y_p", [T, D]); y_s = dout("y_s", [SB * SQ, D])
    sb_p = dout("sb_p", [T, 1024]); sb_s = dout("sb_s", [SB * SQ, 1024])
    nsa_p = dout("nsa_p", [T, 512]); nsa_s = dout("nsa_s", [SB * SQ, 512])
    win_p = dout("win_p", [512, 256]); win_s = dout("win_s", [SB, 512, 256])
    if debug:
        u_pd = dout("u_pd", [T, D]); u_sd = dout("u_sd", [SB * SQ, D])

    with contextlib.ExitStack() as top:
        S = Sched(nc, top)
        E = S.emit

        uniq = [0]

        def sb_t(st, name, shape, dt):
            uniq[0] += 1
            return st.enter_context(nc.sbuf_tensor(f"s{uniq[0]}_" + name, list(shape), dt))

        def ps_t(st, name, shape, dt):
            return st.enter_context(nc.psum_tensor("p_" + name, list(shape), dt))

        pf = [ps_t(top, f"pf{i}", [128, 512], F32) for i in range(6)]
        pb = [ps_t(top, f"pb{i}", [128, 1024], BF16) for i in range(2)]

        if allgather:
            rg = [list(range(NCORE))]
            nrow = sh * 128 * 128 // 512
            for (src, loc_t, full_t) in ag_list:
                srcv = src.rearrange("(a r) c -> a (r c)", r=4)
                locv = loc_t.ap().rearrange("(a r) c -> a (r c)", r=4)
                for r0 in range(0, nrow, 1024):
                    r1 = min(nrow, r0 + 1024)
                    E('pool', lambda e: e.dma_start(out=locv[r0:r1, :], in_=srcv[r0:r1, :]), writes=['agloc'], dma='ag0')
                S.barrier()
                E('pool', lambda e: e.collective_compute("AllGather", ALU.bypass, replica_groups=rg,
                                                         ins=[loc_t.ap().opt()], outs=[full_t.ap().opt()]),
                  reads=['agloc'], writes=['cache_sb', 'cache_nsa'], dma='ag1', inc=1)
            S.barrier()

        cidx = [0]

        def load_const(dst_ap, src_ap, res, cast=False):
            cidx[0] += 1
            E('pool' if cast else 'sp', lambda e: e.dma_start(out=dst_ap, in_=src_ap), writes=[res], dma=f'c{cidx[0] % 4}')

        identf = sb_t(top, "identf", [128, 128], F32)
        ident = sb_t(top, "ident", [128, 128], BF16)
        tri = sb_t(top, "tri", [128, 128], BF16); cotri = sb_t(top, "cotri", [128, 128], BF16)
        onesm = sb_t(top, "onesm", [128, 128], BF16); zerom = sb_t(top, "zerom", [128, 128], BF16)
        sneg = sb_t(top, "sneg", [128, 128], BF16); cneg = sb_t(top, "cneg", [128, 128], BF16)
        bneg = sb_t(top, "bneg", [128, 128], BF16)
        sneg8 = sb_t(top, "sneg8", [128, 64], BF16); cneg8 = sb_t(top, "cneg8", [128, 8], BF16)
        bneg8 = sb_t(top, "bneg8", [128, 8], BF16)
        iotap = sb_t(top, "iotap", [128, 1], F32)
        qg = sb_t(top, "qg", [128, 64], F32); kcg = sb_t(top, "kcg", [128, 64], F32)
        ksg = sb_t(top, "ksg", [128, 64], F32); kwg = sb_t(top, "kwg", [128, 64], F32)
        load_const(identf[:], cd["ident"], 'identf')
        for (tl, nm) in [(ident, "ident"), (tri, "tri"), (cotri, "cotri"), (onesm, "onesm"), (sneg, "sneg"),
                         (cneg, "cneg"), (bneg, "bneg"), (sneg8, "sneg8"), (cneg8, "cneg8"), (bneg8, "bneg8")]:
            load_const(tl[:], cd[nm], nm, cast=True)
        E('dve', lambda e: e.memset(zerom[:], 0.0), writes=['zerom'])
        zrhs = sb_t(top, "zrhs", [128, 512], BF16)
        E('dve', lambda e: e.memset(zrhs[:], 0.0), writes=['zrhs'])
        load_const(iotap[:], cd["iota_p"], 'iotap')
        load_const(qg[:], qg_b, 'qg'); load_const(kcg[:], kcg_b, 'kcg')
        load_const(ksg[:], ksg_b, 'ksg'); load_const(kwg[:], kwg_b, 'kwg')

        QTa_s = sb_t(top, "QTa_s", [128, 4, SB * SQ], BF16)
        KTa_s = sb_t(top, "KTa_s", [128, 4, SB * SQ], BF16)
        Va_s = sb_t(top, "Va_s", [SQ, SB, 512], BF16)
        ga_s = sb_t(top, "ga_s", [SQ, SB, 512], BF16)
        gb_s = sb_t(top, "gb_s", [SQ, SB, 512], BF16)
        QTb_s = sb_t(top, "QTb_s", [68, SB, 8, SQ], BF16)
        KTsel_s = sb_t(top, "KTsel_s", [68, SB, 2, SQ], BF16)
        KTwin_s = sb_t(top, "KTwin_s", [68, SB, 2, SQ], BF16)
        Vsel_s = sb_t(top, "Vsel_s", [SQ, SB, 2, 65], BF16)
        Vwin_s = sb_t(top, "Vwin_s", [SQ, SB, 2, 65], BF16)
        gates_s = sb_t(top, "gates_s", [SQ, SB, 24], F32)
        ua_s = sb_t(top, "ua_s", [SQ, SB, 512], BF16)
        ub_s = sb_t(top, "ub_s", [SQ, SB, 512], BF16)
        load_const(QTb_s[64:68, :, :, :], cd["aq_s"], 'QTb_s', cast=True)
        for b in range(SB):
            for g in range(2):
                load_const(KTsel_s[64:68, b, g, :], cd["ak"][:, PAST:PAST + 8], 'KTsel_s', cast=True)
                load_const(KTwin_s[64:68, b, g, :], cd["ak"][:, PAST:PAST + 8], 'KTwin_s', cast=True)
        E('dve', lambda e: e.memset(Vsel_s[:, :, :, 64:65], 1.0), writes=['Vsel_s'])
        E('dve', lambda e: e.memset(Vwin_s[:, :, :, 64:65], 1.0), writes=['Vwin_s'])
        S.barrier()

        ua = sb_t(top, "ua", [128, NT, 512], BF16)
        ub = sb_t(top, "ub", [128, NT, 512], BF16)

        tiles = [(128, x_p[i * 128:(i + 1) * 128, :], i * 128) for i in range(NT)]
        tiles += [(SQ, x_s[b * SQ:(b + 1) * SQ, :], T + b * SQ) for b in range(SB)]
        NCOLS = T + SB * SQ

        def rstd_chain(stt, R, n, res, scale):
            E('act', lambda e: e.activation(out=stt[0:R, n:2 * n], in_=stt[0:R, 0:n], func=AF.Ln, scale=scale, bias=EPS),
              reads=[res], writes=[res])
            E('act', lambda e: e.activation(out=stt[0:R, 2 * n:3 * n], in_=stt[0:R, n:2 * n], func=AF.Exp, scale=-0.5),
              reads=[res], writes=[res])

        def sigmoid_from(z_ap, tmp_ap, R, zres, tres):
            E('act', lambda e: e.activation(out=tmp_ap, in_=z_ap, func=AF.Exp, scale=-1.0), reads=zres, writes=[tres])
            E('dve', lambda e: e.tensor_scalar(tmp_ap, tmp_ap, 1.0, None, op0=ALU.add), reads=[tres], writes=[tres])
            E('dve', lambda e: e.reciprocal(tmp_ap, tmp_ap), reads=[tres], writes=[tres])

        def make_hT(ph1):
            hT = sb_t(ph1, "hT", [128, 8, NCOLS], BF16)
            if 'inp' in skip:
                return hT
            with contextlib.ExitStack() as st:
                xt = [sb_t(st, f"xt{i}", [128, D], F32) for i in range(2)]
                gb_t = sb_t(st, "gbt", [128, D], F32)
                load_const(gb_t[:], g_b, 'gbt')
                S.barrier()
                junk = sb_t(st, "junk", [128, D], BF16)
                hn = [sb_t(st, f"hn{i}", [128, D], BF16) for i in range(2)]
                ss = [sb_t(st, f"ss{i}", [128, 4], F32) for i in range(2)]
                for ti, (R, xsrc, c0) in enumerate(tiles):
                    p = ti % 2
                    pT = pb[p][:, :].rearrange("p (k c) -> p k c", k=8)
                    E('sp', lambda e: e.dma_start(out=xt[p][0:R, :], in_=xsrc), writes=[f'xt{p}'], dma=f'xt{p}')
                    E('act', lambda e: e.activation(out=junk[0:R, :], in_=xt[p][0:R, :], func=AF.Square,
                                                    accum_out=ss[p][0:R, 0:1]),
                      reads=[f'xt{p}'], writes=['junk', f'ss{p}'])
                    rstd_chain(ss[p], R, 1, f'ss{p}', 1.0 / D)
                    E('dve', lambda e: e.scalar_tensor_tensor(out=hn[p][0:R, :], in0=xt[p][0:R, :], scalar=ss[p][0:R, 2:3],
                                                              in1=gb_t[0:R, :], op0=ALU.mult, op1=ALU.mult),
                      reads=[f'xt{p}', f'ss{p}', 'gbt'], writes=[f'hn{p}'])
                    for k in range(8):
                        E('pe', lambda e: e.transpose(pT[:, k, 0:R], hn[p][0:R, k * 128:(k + 1) * 128], ident[0:R, 0:R]),
                          reads=[f'hn{p}', 'ident'], writes=[f'pb{p}'])
                    E('dve', lambda e: e.tensor_copy(hT[:, :, c0:c0 + R], pT[:, :, 0:R]), reads=[f'pb{p}'], writes=['hT'])
            S.barrier()
            return hT

        if True:
            def inproj(hT, groups, dst):
                if 'inp' in skip:
                    return
                with contextlib.ExitStack() as st:
                    wbf = [sb_t(st, f"wbf{i}", [128, 8, 512], BF16) for i in range(2)]
                    rows = [sb_t(st, f"rows{i}", [128, 512], F32) for i in range(2)]
                    tm = [sb_t(st, f"tm{i}", [128, 512], BF16) for i in range(2)]
                    sq = sb_t(st, "sq", [128, 512], F32)
                    stt = [sb_t(st, f"stt{i}", [128, 24], F32) for i in range(2)]
                    step = 0
                    for gi, gname in enumerate(groups):
                        gc0, gw = GROUPS[gname]
                        wp = gi % 2
                        E('pool', lambda e: e.dma_start(out=wbf[wp][:, :, 0:gw], in_=w_in[:, :, gc0:gc0 + gw]),
                          writes=[f'wbf{wp}'], dma=f'wbf{wp}')
                        for ti, (R, xsrc, c0) in enumerate(tiles):
                            zpar = step % 2
                            step += 1
                            z = pf[zpar]
                            zr = [f'pf{zpar}']
                            pbn = f'pb{zpar}'
                            for k in range(8):
                                E('pe', lambda e: e.matmul(z[0:R, 0:gw], hT[:, k, c0:c0 + R], wbf[wp][:, k, 0:gw],
                                                           start=(k == 0), stop=(k == 7)),
                                  reads=['hT', f'wbf{wp}'], writes=zr)
                            is_p = ti < NT
                            b = ti - NT
                            r0 = ti * 128 if is_p else b * SQ
                            rw = rows[zpar]; rwn = f'rows{zpar}'
                            tmb = tm[zpar]; tmn = f'tm{zpar}'
                            s4 = stt[zpar]; s4n = f'stt{zpar}'

                            def store(dst_ap, src_ap):
                                E('pool', lambda e: e.dma_start(out=dst_ap, in_=src_ap), reads=[rwn], dma=f'o_rows{zpar}')

                            def tr_pairs(dst_ap, dres):
                                pT = pb[zpar][:, :].rearrange("p (k c) -> p k c", k=8)
                                for j in range(4):
                                    E('pe', lambda e: e.transpose(pT[:, j, 0:R], tmb[0:R, j * 128:(j + 1) * 128], ident[0:R, 0:R]),
                                      reads=[tmn, 'ident'], writes=[pbn])
                                E('act', lambda e: e.activation(out=dst_ap, in_=pT[:, 0:4, 0:R], func=AF.Copy),
                                  reads=[pbn], writes=[dres])

                            def norm_pair(col0, gain, gname_):
                                E('act', lambda e: e.activation(out=sq[0:R, 0:128], in_=z[0:R, col0:col0 + 128], func=AF.Square),
                                  reads=zr, writes=['sq'])
                                E('dve', lambda e: e.tensor_reduce(out=s4[0:R, 0:2], in_=sq[0:R, 0:128].rearrange("p (g d) -> p g d", g=2),
                                                                   axis=AX.X, op=ALU.add), reads=['sq'], writes=[s4n])
                                rstd_chain(s4, R, 2, s4n, 1.0 / 64)
                                for g in range(2):
                                    E('dve', lambda e: e.scalar_tensor_tensor(
                                        out=rw[0:R, col0 + 64 * g:col0 + 64 + 64 * g], in0=z[0:R, col0 + 64 * g:col0 + 64 + 64 * g],
                                        scalar=s4[0:R, 4 + g:5 + g], in1=gain[0:R, :], op0=ALU.mult, op1=ALU.mult),
                                      reads=zr + [s4n, gname_], writes=[rwn])

                            def tr_heads64(src_ap_fn, nheads, dst_ap, dres):
                                pT = pb[zpar][:, :].rearrange("p (k c) -> p k c", k=8)
                                for h in range(nheads):
                                    E('pe', lambda e: e.transpose(pT[0:64, h, 0:R], src_ap_fn(h), ident[0:R, 0:R]),
                                      reads=[tmn, 'ident'], writes=[pbn])
                                E('act', lambda e: e.activation(out=dst_ap, in_=pT[0:64, 0:nheads, 0:R], func=AF.Copy),
                                  reads=[pbn], writes=[dres])

                            if gname == "sbq":
                                E('dve', lambda e: e.tensor_copy(tmb[0:R, :], z[0:R, :]), reads=zr, writes=[tmn])
                                if is_p:
                                    tr_pairs(dst["QTa"][:, :, c0:c0 + R], 'QTa')
                                else:
                                    tr_pairs(QTa_s[:, :, b * SQ:(b + 1) * SQ], 'QTa_s')
                            elif gname == "sbk":
                                E('act', lambda e: e.activation(out=rw[0:R, :], in_=z[0:R, :], func=AF.Copy), reads=zr, writes=[rwn])
                                store((sb_p if is_p else sb_s)[r0:r0 + R, 0:512], rw[0:R, :])
                                E('dve', lambda e: e.tensor_copy(tmb[0:R, :], rw[0:R, :]), reads=[rwn], writes=[tmn])
                                if is_p:
                                    tr_pairs(dst["KTa"][:, :, c0:c0 + R], 'KTa')
                                else:
                                    tr_pairs(KTa_s[:, :, b * SQ:(b + 1) * SQ], 'KTa_s')
                            elif gname == "sbv":
                                E('act', lambda e: e.activation(out=rw[0:R, :], in_=z[0:R, :], func=AF.Copy), reads=zr, writes=[rwn])
                                store((sb_p if is_p else sb_s)[r0:r0 + R, 512:1024], rw[0:R, :])
                                if is_p:
                                    E('dve', lambda e: e.tensor_copy(dst["Va"][:, ti, :], rw[0:R, :]), reads=[rwn], writes=['Va'])
                                else:
                                    E('dve', lambda e: e.tensor_copy(Va_s[:, b, :], rw[0:R, :]), reads=[rwn], writes=['Va_s'])
                            elif gname in ("sbg", "ng"):
                                sigmoid_from(z[0:R, :], sq[0:R, :], R, zr, 'sq')
                                if gname == "sbg":
                                    o_ap, ores = (dst["ga"][:, ti, :], 'ga') if is_p else (ga_s[:, b, :], 'ga_s')
                                else:
                                    o_ap, ores = (dst["gb"][:, ti, :], 'gb') if is_p else (gb_s[:, b, :], 'gb_s')
                                E('dve', lambda e: e.tensor_tensor(out=o_ap, in0=z[0:R, :], in1=sq[0:R, :], op=ALU.mult),
                                  reads=zr + ['sq'], writes=[ores])
                            elif gname == "nq":
                                E('act', lambda e: e.activation(out=sq[0:R, :], in_=z[0:R, :], func=AF.Square), reads=zr, writes=['sq'])
                                E('dve', lambda e: e.tensor_reduce(out=s4[0:R, 0:8], in_=sq[0:R, :].rearrange("p (h d) -> p h d", h=8),
                                                                   axis=AX.X, op=ALU.add), reads=['sq'], writes=[s4n])
                                rstd_chain(s4, R, 8, s4n, 1.0 / 64)
                                E('dve', lambda e: e.tensor_tensor(out=sq[0:R, :].rearrange("p (h d) -> p h d", h=8),
                                                                   in0=z[0:R, :].rearrange("p (h d) -> p h d", h=8),
                                                                   in1=s4[0:R, 16:24].unsqueeze(2).broadcast_to([R, 8, 64]), op=ALU.mult),
                                  reads=zr + [s4n], writes=['sq'])
                                E('dve', lambda e: e.tensor_tensor(out=tmb[0:R, :].rearrange("p (h d) -> p h d", h=8),
                                                                   in0=sq[0:R, :].rearrange("p (h d) -> p h d", h=8),
                                                                   in1=qg[0:R, :].unsqueeze(1).broadcast_to([R, 8, 64]), op=ALU.mult),
                                  reads=['sq', 'qg'], writes=[tmn])
                                if is_p:
                                    tr_heads64(lambda h: tmb[0:R, h * 64:(h + 1) * 64], 8, dst["QTb"][0:64, ti, :, :], 'QTb')
                                else:
                                    tr_heads64(lambda h: tmb[0:R, h * 64:(h + 1) * 64], 8, QTb_s[0:64, b, :, :], 'QTb_s')
                            elif gname == "ncs":
                                E('act', lambda e: e.activation(out=rw[0:R, 0:256], in_=z[0:R, 0:256], func=AF.Copy), reads=zr, writes=[rwn])
                                E('act', lambda e: e.activation(out=rw[0:R, 384:512], in_=z[0:R, 384:512], func=AF.Copy), reads=zr, writes=[rwn])
                                norm_pair(256, ksg, 'ksg')
                                store((nsa_p if is_p else nsa_s)[r0:r0 + R, :], rw[0:R, :])
                                E('dve', lambda e: e.tensor_copy(tmb[0:R, :], rw[0:R, :]), reads=[rwn], writes=[tmn])
                                pT = pb[zpar][:, :].rearrange("p (k c) -> p k c", k=8)
                                if is_p:
                                    for j in range(2):
                                        E('pe', lambda e: e.transpose(pT[:, j, 0:R], tmb[0:R, j * 128:(j + 1) * 128], ident[0:R, 0:R]),
                                          reads=[tmn, 'ident'], writes=[pbn])
                                    E('act', lambda e: e.activation(out=dst["XT"][:, :, c0:c0 + R], in_=pT[:, 0:2, 0:R], func=AF.Copy),
                                      reads=[pbn], writes=['XT'])
                                    tr_heads64(lambda g: tmb[0:R, 256 + g * 64:320 + g * 64], 2, dst["KTsel"][0:64, :, c0:c0 + R], 'KTsel')
                                    E('dve', lambda e: e.tensor_copy(dst["Vsel"][:, ti, :, 0:64],
                                                                     rw[0:R, 384:512].rearrange("p (g d) -> p g d", g=2)),
                                      reads=[rwn], writes=['Vsel'])
                                else:
                                    tr_heads64(lambda g: tmb[0:R, 256 + g * 64:320 + g * 64], 2, KTsel_s[0:64, b, :, :], 'KTsel_s')
                                    E('dve', lambda e: e.tensor_copy(Vsel_s[:, b, :, 0:64],
                                                                     rw[0:R, 384:512].rearrange("p (g d) -> p g d", g=2)),
                                      reads=[rwn], writes=['Vsel_s'])
                            elif gname == "nwb":
                                E('act', lambda e: e.activation(out=rw[0:R, 128:256], in_=z[0:R, 128:256], func=AF.Copy), reads=zr, writes=[rwn])
                                norm_pair(0, kwg, 'kwg')
                                if is_p:
                                    if ti >= NT - 4:
                                        store(win_p[(ti - (NT - 4)) * 128:(ti - (NT - 4) + 1) * 128, :], rw[0:R, 0:256])
                                else:
                                    store(win_s[b, 504:512, :], rw[0:R, 0:256])
                                E('dve', lambda e: e.tensor_copy(tmb[0:R, 0:128], rw[0:R, 0:128]), reads=[rwn], writes=[tmn])
                                if is_p:
                                    tr_heads64(lambda g: tmb[0:R, g * 64:64 + g * 64], 2, dst["KTwin"][0:64, :, c0:c0 + R], 'KTwin')
                                    E('dve', lambda e: e.tensor_copy(dst["Vwin"][:, ti, :, 0:64],
                                                                     rw[0:R, 128:256].rearrange("p (g d) -> p g d", g=2)),
                                      reads=[rwn], writes=['Vwin'])
                                    g_ap, gres = dst["gates"][:, ti, :], 'gates'
                                else:
                                    tr_heads64(lambda g: tmb[0:R, g * 64:64 + g * 64], 2, KTwin_s[0:64, b, :, :], 'KTwin_s')
                                    E('dve', lambda e: e.tensor_copy(Vwin_s[:, b, :, 0:64],
                                                                     rw[0:R, 128:256].rearrange("p (g d) -> p g d", g=2)),
                                      reads=[rwn], writes=['Vwin_s'])
                                    g_ap, gres = gates_s[:, b, :], 'gates_s'
                                sigmoid_from(z[0:R, 256:280], g_ap, R, zr, gres)

            with contextlib.ExitStack() as sbst:
                hT = make_hT(sbst)
                QTa = sb_t(sbst, "QTa", [128, 4, T], BF16)
                KTa = sb_t(sbst, "KTa", [128, 4, T], BF16)
                Va = sb_t(sbst, "Va", [128, NT, 512], BF16)
                ga = sb_t(sbst, "ga", [128, NT, 512], BF16)
                inproj(hT, ["sbq", "sbk", "sbv", "sbg"], dict(QTa=QTa, KTa=KTa, Va=Va, ga=ga))
                S.barrier()
                with contextlib.ExitStack() as st:
                    e32 = [sb_t(st, f"e32{i}", [128, 512], F32) for i in range(2)]
                    spb = [sb_t(st, f"spb{i}", [128, 512], BF16) for i in range(2)]
                    e2 = [sb_t(st, f"e2{i}", [128, 512], F32) for i in range(2)]
                    ab = [sb_t(st, f"ab{i}", [128, 512], BF16) for i in range(2)]
                    step = 0

                    def sb_chain(A, An, Cacc, Cn, K, n_, par, c0=0):
                        sl_ = slice(c0, c0 + n_)
                        E('act', lambda e: e.activation(out=e32[par][0:K, sl_], in_=A[0:K, sl_], func=AF.Exp, scale=0.125),
                          reads=[An], writes=[f'e32{par}'])
                        E('act', lambda e: e.activation(out=spb[par][0:K, sl_], in_=e32[par][0:K, sl_], func=AF.Ln, scale=1.0, bias=1.0),
                          reads=[f'e32{par}'], writes=[f'spb{par}'])
                        E('pe', lambda e: e.matmul(Cacc[0:K, sl_], tri[0:K, 0:K], spb[par][0:K, sl_], start=False, stop=True,
                                                   skip_group_check=True),
                          reads=[f'spb{par}', 'tri'], writes=[Cn])
                        E('act', lambda e: e.activation(out=e2[par][0:K, sl_], in_=Cacc[0:K, sl_], func=AF.Exp, scale=-1.0),
                          reads=[Cn], writes=[f'e2{par}'])
                        E('dve', lambda e: e.tensor_tensor(out=ab[par][0:K, sl_], in0=e32[par][0:K, sl_], in1=e2[par][0:K, sl_], op=ALU.mult),
                          reads=[f'e32{par}', f'e2{par}'], writes=[f'ab{par}'])

                    for h in (range(8) if 'sbp' not in skip else ()):
                        j, base = h // 2, 64 * (h % 2)
                        for Q in range(4):
                            hq = (h * 4 + Q) % 2
                            Cacc, Cn = pf[2 + hq], f'pf{2 + hq}'
                            O, On = pf[4 + hq], f'pf{4 + hq}'
                            Ov = O[:, 0:256].rearrange("p (j d) -> p j d", j=4)
                            E('pe', lambda e: e.matmul(Cacc[:, :], zerom[:, :], zrhs[:, :], start=True, stop=True, skip_group_check=True),
                              reads=['zerom', 'zrhs'], writes=[Cn])
                            E('pe', lambda e: e.matmul(O[:, 0:256], zerom[:, :], zrhs[:, 0:256], start=True, stop=True, skip_group_check=True),
                              reads=['zerom', 'zrhs'], writes=[On])
                            for kt in range(4 * Q + 3, -1, -1):
                                par = step % 2
                                step += 1
                                A, An = pf[par], f'pf{par}'
                                jd = kt - 4 * Q
                                c0 = max(0, jd) * 128
                                n_ = 512 - c0
                                sl_ = slice(c0, 512)
                                E('pe', lambda e: e.matmul(A[:, sl_], KTa[base:base + 64, j, kt * 128:(kt + 1) * 128],
                                                           QTa[base:base + 64, j, Q * 512 + c0:Q * 512 + 512], start=True, stop=(jd < 0)),
                                  reads=['KTa', 'QTa'], writes=[An])
                                if jd >= 0:
                                    E('pe', lambda e: e.matmul(A[:, c0:c0 + 128], ident[:, :], sneg[:, :], start=False, stop=True),
                                      reads=['ident', 'sneg'], writes=[An])
                                sb_chain(A, An, Cacc, Cn, 128, n_, par, c0)
                                E('pe', lambda e: e.matmul(Cacc[:, sl_], cotri[:, :], spb[par][:, sl_], start=False, stop=True,
                                                           skip_group_check=True),
                                  reads=[f'spb{par}', 'cotri'], writes=[Cn])
                                for jj in range(max(0, jd), 4):
                                    E('pe', lambda e: e.matmul(Ov[:, jj, :], ab[par][:, jj * 128:(jj + 1) * 128],
                                                               Va[:, kt, h * 64:(h + 1) * 64],
                                                               start=False, stop=True, skip_group_check=True),
                                      reads=[f'ab{par}', 'Va'], writes=[On])
                            E('dve', lambda e: e.tensor_tensor(out=ua[:, 4 * Q:4 * Q + 4, h * 64:(h + 1) * 64], in0=Ov,
                                                               in1=ga[:, 4 * Q:4 * Q + 4, h * 64:(h + 1) * 64], op=ALU.mult),
                              reads=[On, 'ga'], writes=['ua'])

                    Qbd = sb_t(st, "Qbd", [128, 4, 64], BF16)
                    idxf = sb_t(st, "idxf", [128, NPAGE], F32)
                    idxi = sb_t(st, "idxi", [128, NPAGE], I32)
                    ptb = sb_t(st, "ptb", [128, NPAGE], I32)
                    kv = [sb_t(st, f"kv{i}", [128, 1024], BF16) for i in range(2)]
                    KTp = [sb_t(st, f"KTp{i}", [128, 4, 128], BF16) for i in range(2)]
                    for b in (range(SB) if 'sbs' not in skip else ()):
                        E('dve', lambda e: e.memset(Qbd[:], 0.0), writes=['Qbd'])
                        for j in range(4):
                            for hh in range(2):
                                h = 2 * j + hh
                                E('dve', lambda e: e.tensor_copy(Qbd[64 * hh:64 * hh + 64, j, h * 8:(h + 1) * 8],
                                                                 QTa_s[64 * hh:64 * hh + 64, j, b * SQ:(b + 1) * SQ]),
                                  reads=['QTa_s'], writes=['Qbd'])
                        E('sp', lambda e: e.dma_start(out=ptb[:], in_=ptab[b:b + 1, :].partition_broadcast(128)), writes=['ptb'], dma='ptb')
                        E('dve', lambda e: e.tensor_copy(idxf[:], ptb[:]), reads=['ptb'], writes=['idxf'])
                        E('dve', lambda e: e.tensor_scalar(idxf[:], idxf[:], 128.0, iotap[:, 0:1], op0=ALU.mult, op1=ALU.add),
                          reads=['idxf', 'iotap'], writes=['idxf'])
                        E('dve', lambda e: e.tensor_copy(idxi[:], idxf[:]), reads=['idxf'], writes=['idxi'])
                        hq = b % 2
                        Cacc, Cn = pf[2 + hq], f'pf{2 + hq}'
                        O, On = pf[4 + hq], f'pf{4 + hq}'
                        Cnew, Cnn = pf[2 + (1 - hq)], f'pf{2 + (1 - hq)}'
                        par = step % 2
                        step += 1
                        A, An = pf[par], f'pf{par}'
                        for j in range(4):
                            E('pe', lambda e: e.matmul(A[0:8, 0:64], KTa_s[:, j, b * SQ:(b + 1) * SQ], Qbd[:, j, :], start=(j == 0), stop=False),
                              reads=['KTa_s', 'Qbd'], writes=[An])
                        E('pe', lambda e: e.matmul(A[0:8, 0:64], ident[0:8, 0:8], sneg8[0:8, :], start=False, stop=True),
                          reads=['ident', 'sneg8'], writes=[An])
                        E('pe', lambda e: e.matmul(Cnew[0:8, 0:64], zerom[0:8, 0:8], sneg8[0:8, :], start=True, stop=True, skip_group_check=True),
                          reads=['zerom'], writes=[Cnn])
                        sb_chain(A, An, Cnew, Cnn, 8, 64, par)
                        E('pe', lambda e: e.matmul(Cacc[:, 0:64], onesm[0:8, :], spb[par][0:8, 0:64], start=True, stop=True, skip_group_check=True),
                          reads=[f'spb{par}', 'onesm'], writes=[Cn])
                        E('pe', lambda e: e.matmul(O[0:8, 0:512], zerom[:, 0:8], zrhs[:, 0:512], start=True, stop=True, skip_group_check=True),
                          reads=['zerom', 'zrhs'], writes=[On])
                        for h in range(8):
                            E('pe', lambda e: e.matmul(O[0:8, h * 64:(h + 1) * 64], ab[par][0:8, h * 8:(h + 1) * 8],
                                                       Va_s[0:8, b, h * 64:(h + 1) * 64], start=False, stop=True, skip_group_check=True),
                              reads=[f'ab{par}', 'Va_s'], writes=[On])
                        for pg in range(NPAGE - 1, -1, -1):
                            par = step % 2
                            step += 1
                            A, An = pf[par], f'pf{par}'
                            kvt, kvn = kv[par], f'kv{par}'
                            for (cap, clo, cw) in sb_chunks:
                                E('pool', lambda e: e.indirect_dma_start(out=kvt[:, clo:clo + cw], out_offset=None, in_=cap,
                                                                         in_offset=bass.IndirectOffsetOnAxis(ap=idxi[:, pg:pg + 1], axis=0)),
                                  reads=['idxi', 'cache_sb'], writes=[kvn], dma=kvn)
                            pT = pb[par][:, :].rearrange("p (k c) -> p k c", k=8)
                            for j in range(4):
                                E('pe', lambda e: e.transpose(pT[:, j, :], kvt[:, j * 128:(j + 1) * 128], ident[:, :]),
                                  reads=[kvn, 'ident'], writes=[f'pb{par}'])
                            E('dve', lambda e: e.tensor_copy(KTp[par][:], pT[:, 0:4, :]), reads=[f'pb{par}'], writes=[f'KTp{par}'])
                            for j in range(4):
                                E('pe', lambda e: e.matmul(A[:, 0:64], KTp[par][:, j, :], Qbd[:, j, :], start=(j == 0), stop=(j == 3)),
                                  reads=[f'KTp{par}', 'Qbd'], writes=[An])
                            sb_chain(A, An, Cacc, Cn, 128, 64, par)
                            E('pe', lambda e: e.matmul(Cacc[:, 0:64], cotri[:, :], spb[par][:, 0:64], start=False, stop=True, skip_group_check=True),
                              reads=[f'spb{par}', 'cotri'], writes=[Cn])
                            for h in range(8):
                                E('pe', lambda e: e.matmul(O[0:8, h * 64:(h + 1) * 64], ab[par][:, h * 8:(h + 1) * 8],
                                                           kvt[:, 512 + h * 64:512 + (h + 1) * 64], start=False, stop=True,
                                                           skip_group_check=True),
                                  reads=[f'ab{par}', kvn], writes=[On])
                        E('dve', lambda e: e.tensor_tensor(out=ua_s[:, b, :], in0=O[0:8, :], in1=ga_s[:, b, :], op=ALU.mult),
                          reads=[On, 'ga_s'], writes=['ua_s'])
                S.barrier()

        if True:
            nsast = contextlib.ExitStack()
            QTb = sb_t(nsast, "QTb", [68, NT, 8, 128], BF16)
            gbp = sb_t(nsast, "gbp", [128, NT, 512], BF16)
            XT = sb_t(nsast, "XT", [128, 2, T], BF16)
            KTsel = sb_t(nsast, "KTsel", [68, 2, T], BF16)
            KTwin = sb_t(nsast, "KTwin", [68, 2, T], BF16)
            Vsel = sb_t(nsast, "Vsel", [128, NT, 2, 65], BF16)
            Vwin = sb_t(nsast, "Vwin", [128, NT, 2, 65], BF16)
            gates = sb_t(nsast, "gates", [128, NT, 24], F32)
            load_const(QTb[64:68, :, :, :], cd["aq_p"], 'QTb', cast=True)
            for g in range(2):
                load_const(KTsel[64:68, g, :], cd["ak"][:, 0:T], 'KTsel', cast=True)
                load_const(KTwin[64:68, g, :], cd["ak"][:, 0:T], 'KTwin', cast=True)
            E('dve', lambda e: e.memset(Vsel[:, :, :, 64:65], 1.0), writes=['Vsel'])
            E('dve', lambda e: e.memset(Vwin[:, :, :, 64:65], 1.0), writes=['Vwin'])
            S.barrier()
            with contextlib.ExitStack() as hst:
                hT = make_hT(hst)
                inproj(hT, ["nq", "ncs", "nwb", "ng"], dict(QTb=QTb, gb=gbp, XT=XT, KTsel=KTsel, KTwin=KTwin, Vsel=Vsel, Vwin=Vwin, gates=gates))
                S.barrier()
        S.barrier()

        def nsa_phase(which):
          with contextlib.ExitStack() as nst:
            NCMP = 64 if which == 'prompt' else 256
            NBLK = NCMP // 2

            def make_compress(cs):
                W1 = [sb_t(cs, f"W1{i}", [128, 32, 256], BF16) for i in range(2)]
                W2 = [sb_t(cs, f"W2{i}", [128, 2, 64], BF16) for i in range(2)]
                peT = [sb_t(cs, f"peT{i}", [64, 32], BF16) for i in range(2)]
                b1 = sb_t(cs, "b1", [128, 2, 2], F32)
                load_const(W1[0][:], w1k_d, 'W1', cast=True); load_const(W1[1][:], w1v_d, 'W1', cast=True)
                load_const(W2[0][:], w2k_d, 'W2', cast=True); load_const(W2[1][:], w2v_d, 'W2', cast=True)
                load_const(peT[0][:], pek_d, 'peT', cast=True); load_const(peT[1][:], pev_d, 'peT', cast=True)
                S.barrier()
                for kind in range(2):
                    for jc in range(2):
                        for l in range(32):
                            E('pe', lambda e: e.matmul(pf[0][:, kind * 2 + jc:kind * 2 + jc + 1], W1[kind][0:64, l, jc * 128:(jc + 1) * 128],
                                                       peT[kind][:, l:l + 1], start=(l == 0), stop=(l == 31), skip_group_check=True),
                              reads=['W1', 'peT'], writes=['pf0'])
                E('dve', lambda e: e.tensor_copy(b1[:].rearrange("p k j -> p (k j)"), pf[0][:, 0:4]), reads=['pf0'], writes=['b1'])

                hb = sb_t(cs, "hb", [128, 2 * NCMP], F32)
                hsg = sb_t(cs, "hsg", [128, 2 * NCMP], F32)
                HT = sb_t(cs, "HT", [128, 2, NCMP], BF16)
                kcn = sb_t(cs, "kcn", [128, 128], BF16)
                ktmp = sb_t(cs, "ktmp", [128, 128], F32)
                cst = sb_t(cs, "cst", [128, 8], F32)

                def compress(xt_fn, ncmp, kcT_dst, vc_dst, kres, vres):
                    nct = (ncmp + 127) // 128
                    for kind in range(2):
                        for g in range(2):
                            Hps = pf[1][:, 0:2 * ncmp].rearrange("p (j c) -> p j c", j=2)
                            for jc in range(2):
                                for l in range(32):
                                    E('pe', lambda e: e.matmul(Hps[:, jc, :], W1[kind][64 * g:64 * g + 64, l, jc * 128:(jc + 1) * 128],
                                                               xt_fn(kind, g, l), start=(l == 0), stop=(l == 31), skip_group_check=True),
                                      reads=['W1', 'XTsrc'], writes=['pf1'])
                            hbv = hb[:, 0:2 * ncmp].rearrange("p (j c) -> p j c", j=2)
                            hsv = hsg[:, 0:2 * ncmp].rearrange("p (j c) -> p j c", j=2)
                            for jc in range(2):
                                E('dve', lambda e: e.tensor_scalar(hbv[:, jc, :], Hps[:, jc, :], b1[:, kind, jc:jc + 1], None, op0=ALU.add),
                                  reads=['pf1', 'b1'], writes=['hb'])
                            sigmoid_from(hb[:, 0:2 * ncmp], hsg[:, 0:2 * ncmp], 128, ['hb'], 'hsg')
                            E('dve', lambda e: e.tensor_tensor(out=HT[:, :, 0:ncmp], in0=hbv, in1=hsv, op=ALU.mult),
                              reads=['hb', 'hsg'], writes=['HT'])
                            for ct in range(nct):
                                cn_ = min(128, ncmp - ct * 128)
                                for jc in range(2):
                                    E('pe', lambda e: e.matmul(pf[0][0:cn_, 0:64], HT[:, jc, ct * 128:ct * 128 + cn_], W2[kind][:, jc, :],
                                                               start=(jc == 0), stop=(jc == 1)),
                                      reads=['HT', 'W2'], writes=['pf0'])
                                if kind == 0:
                                    E('act', lambda e: e.activation(out=ktmp[0:cn_, 0:64], in_=pf[0][0:cn_, 0:64], func=AF.Square,
                                                                    accum_out=cst[0:cn_, 0:1]), reads=['pf0'], writes=['ktmp', 'cst'])
                                    rstd_chain(cst, cn_, 1, 'cst', 1.0 / 64)
                                    E('dve', lambda e: e.scalar_tensor_tensor(out=kcn[0:cn_, 0:64], in0=pf[0][0:cn_, 0:64], scalar=cst[0:cn_, 2:3],
                                                                              in1=kcg[0:cn_, :], op0=ALU.mult, op1=ALU.mult),
                                      reads=['pf0', 'cst', 'kcg'], writes=['kcn'])
                                    E('pe', lambda e: e.transpose(pb[0][0:64, 0:cn_], kcn[0:cn_, 0:64], ident[0:cn_, 0:cn_]),
                                      reads=['kcn', 'ident'], writes=['pb0'])
                                    E('act', lambda e: e.activation(out=kcT_dst[0:64, g, ct * 128:ct * 128 + cn_], in_=pb[0][0:64, 0:cn_], func=AF.Copy),
                                      reads=['pb0'], writes=[kres])
                                else:
                                    E('act', lambda e: e.activation(out=vc_dst(ct)[0:cn_, g, :], in_=pf[0][0:cn_, 0:64], func=AF.Copy),
                                      reads=['pf0'], writes=[vres])
                return compress

            sbc = sb_t(nst, "sbc", [128, 4 * NCMP], F32)
            pbf = sb_t(nst, "pbf", [128, 4 * NCMP], BF16)
            mx = sb_t(nst, "mx", [128, 16], F32)
            imp = sb_t(nst, "imp", [128, 2, NBLK], F32)
            sc = sb_t(nst, "sc", [128, 2, NBLK], F32)
            scw = sb_t(nst, "scw", [128, NCMP], F32)
            m8 = sb_t(nst, "m8", [128, 16], F32)
            negq = sb_t(nst, "negq", [128, 2, NBLK], BF16)
            acc = sb_t(nst, "acc", [128, 8, 64], F32)
            fac = sb_t(nst, "fac", [128, 8], F32)
            tmpo = sb_t(nst, "tmpo", [128, 4, 64], F32)
            pTs = sb_t(nst, "pTs", [128, 4 * ((NCMP + 127) // 128) * 128], BF16)
            Pex = [sb_t(nst, f"Pex{i}", [128, 512], BF16) for i in range(2)]
            NEGT = sb_t(nst, "NEGT", [128, 2, 128], BF16)

            def nsa_compressed(R, ncmp, nblk, kth, q_fn, kcT, vc_fn, bias_ap, mul_ap, add_ap, gate_ap, gres):
                nct = (ncmp + 127) // 128
                for g in range(2):
                    Sc = [pf[2 + (r // 2)] for r in range(4)] if ncmp > 128 else [pf[2]] * 4
                    Scn = ['pf2', 'pf3'] if ncmp > 128 else ['pf2']
                    for r in range(4):
                        h = 4 * g + r
                        off = (r % 2) * ncmp if ncmp > 128 else r * ncmp
                        E('pe', lambda e: e.matmul(Sc[r][0:R, off:off + ncmp], q_fn(h), kcT[0:64, g, 0:ncmp], start=True, stop=True),
                          reads=['QTq', 'kcT'], writes=Scn)
                    sv = sbc[0:R, 0:4 * ncmp]
                    for r in range(4):
                        h = 4 * g + r
                        off = (r % 2) * ncmp if ncmp > 128 else r * ncmp
                        E('dve', lambda e: e.scalar_tensor_tensor(out=sbc[0:R, r * ncmp:(r + 1) * ncmp], in0=Sc[r][0:R, off:off + ncmp], scalar=0.125,
                                                                  in1=bias_ap[0:R, h * ncmp:(h + 1) * ncmp], op0=ALU.mult, op1=ALU.add),
                          reads=Scn + ['biasc'], writes=['sbc'])
                    s3 = sv.rearrange("p (r c) -> p r c", r=4)
                    E('dve', lambda e: e.tensor_reduce(out=mx[0:R, 0:4], in_=s3, axis=AX.X, op=ALU.max), reads=['sbc'], writes=['mx'])
                    E('dve', lambda e: e.tensor_scalar(mx[0:R, 0:4], mx[0:R, 0:4], -1e4, None, op0=ALU.max), reads=['mx'], writes=['mx'])
                    E('dve', lambda e: e.tensor_tensor(out=s3, in0=s3, in1=mx[0:R, 0:4].unsqueeze(2).broadcast_to([R, 4, ncmp]), op=ALU.subtract),
                      reads=['sbc', 'mx'], writes=['sbc'])
                    E('act', lambda e: e.activation(out=sv, in_=sv, func=AF.Exp), reads=['sbc'], writes=['sbc'])
                    E('dve', lambda e: e.tensor_reduce(out=mx[0:R, 4:8], in_=s3, axis=AX.X, op=ALU.add), reads=['sbc'], writes=['mx'])
                    E('dve', lambda e: e.tensor_scalar(mx[0:R, 4:8], mx[0:R, 4:8], 1e-30, None, op0=ALU.max), reads=['mx'], writes=['mx'])
                    E('dve', lambda e: e.reciprocal(mx[0:R, 8:12], mx[0:R, 4:8]), reads=['mx'], writes=['mx'])
                    E('dve', lambda e: e.tensor_tensor(out=s3, in0=s3, in1=mx[0:R, 8:12].unsqueeze(2).broadcast_to([R, 4, ncmp]), op=ALU.mult),
                      reads=['sbc', 'mx'], writes=['sbc'])
                    E('act', lambda e: e.activation(out=pbf[0:R, 0:4 * ncmp], in_=sv, func=AF.Copy), reads=['sbc'], writes=['pbf'])
                    E('dve', lambda e: e.tensor_reduce(out=scw[0:R, 0:ncmp], in_=sv.rearrange("p (r c) -> p c r", r=4), axis=AX.X, op=ALU.add),
                      reads=['sbc'], writes=['scw'])
                    E('dve', lambda e: e.tensor_reduce(out=imp[0:R, g, 0:nblk], in_=scw[0:R, 0:ncmp].rearrange("p (b j) -> p b j", j=2),
                                                       axis=AX.X, op=ALU.add), reads=['scw'], writes=['imp'])
                    E('dve', lambda e: e.tensor_tensor(out=sc[0:R, g, 0:nblk], in0=imp[0:R, g, 0:nblk], in1=mul_ap, op=ALU.mult),
                      reads=['imp', 'selm'], writes=['sc'])
                    E('dve', lambda e: e.tensor_tensor(out=sc[0:R, g, 0:nblk], in0=sc[0:R, g, 0:nblk], in1=add_ap, op=ALU.add),
                      reads=['sc', 'selm'], writes=['sc'])
                    E('dve', lambda e: e.max(out=m8[0:R, 0:8], in_=sc[0:R, g, 0:nblk]), reads=['sc'], writes=['m8'])
                    E('dve', lambda e: e.match_replace(out=scw[0:R, 0:nblk], in_to_replace=m8[0:R, 0:8], in_values=sc[0:R, g, 0:nblk], imm_value=-1e9),
                      reads=['sc', 'm8'], writes=['scw'])
                    E('dve', lambda e: e.max(out=m8[0:R, 8:16], in_=scw[0:R, 0:nblk]), reads=['scw'], writes=['m8'])
                    E('dve', lambda e: e.tensor_scalar(scw[0:R, 0:nblk], sc[0:R, g, 0:nblk], m8[0:R, 8 + kth:9 + kth], None, op0=ALU.is_ge),
                      reads=['sc', 'm8'], writes=['scw'])
                    E('dve', lambda e: e.tensor_scalar(negq[0:R, g, 0:nblk], scw[0:R, 0:nblk], -1.0, -NEG, op0=ALU.add, op1=ALU.mult),
                      reads=['scw'], writes=['negq'])
                    E('pe', lambda e: e.transpose(pb[0][0:nblk, g * 128:g * 128 + R], negq[0:R, g, 0:nblk], ident[0:R, 0:R]),
                      reads=['negq', 'ident'], writes=['pb0'])
                    pTv = pb[1][:, 0:4 * nct * 128].rearrange("p (r t q) -> p r t q", r=4, t=nct)
                    for r in range(4):
                        for ct in range(nct):
                            cn_ = min(128, ncmp - ct * 128)
                            E('pe', lambda e: e.transpose(pTv[0:cn_, r, ct, 0:R], pbf[0:R, r * ncmp + ct * 128:r * ncmp + ct * 128 + cn_],
                                                          ident[0:R, 0:R]), reads=['pbf', 'ident'], writes=['pb1'])
                    cpn = min(128, ncmp)
                    pTsv = pTs[:, 0:4 * nct * 128].rearrange("p (r t q) -> p r t q", r=4, t=nct)
                    E('act', lambda e: e.activation(out=pTsv[0:cpn, :, :, 0:R], in_=pTv[0:cpn, :, :, 0:R], func=AF.Copy), reads=['pb1'], writes=['pTs'])
                    Oc = pf[4][:, 0:256].rearrange("p (r d) -> p r d", r=4)
                    for r in range(4):
                        for ct in range(nct):
                            cn_ = min(128, ncmp - ct * 128)
                            E('pe', lambda e: e.matmul(Oc[0:R, r, :], pTsv[0:cn_, r, ct, 0:R], vc_fn(ct)[0:cn_, g, :],
                                                       start=(ct == 0), stop=(ct == nct - 1), skip_group_check=True),
                              reads=['pTs', 'vc'], writes=['pf4'])
                    E('dve', lambda e: e.tensor_tensor(out=acc[0:R, 4 * g:4 * g + 4, :], in0=Oc[0:R, :, :],
                                                       in1=gate_ap[:, 4 * g:4 * g + 4, 0:1].broadcast_to([R, 4, 64]), op=ALU.mult),
                      reads=['pf4', gres], writes=['acc'])
                E('act', lambda e: e.activation(out=NEGT[0:nblk, :, 0:R], in_=pb[0][0:nblk, 0:256].rearrange("p (g q) -> p g q", g=2)[:, :, 0:R],
                                                func=AF.Copy), reads=['pb0'], writes=['NEGT'])

            def branch_finish(R, g, Os, Osn, gate_ap, gres, bi):
                E('dve', lambda e: e.reciprocal(fac[0:R, 0:4], Os[0:R, :, 64]), reads=[Osn], writes=['fac'])
                E('dve', lambda e: e.tensor_tensor(out=fac[0:R, 4:8], in0=fac[0:R, 0:4], in1=gate_ap[:, 4 * g:4 * g + 4, bi], op=ALU.mult),
                  reads=['fac', gres], writes=['fac'])
                E('dve', lambda e: e.tensor_tensor(out=tmpo[0:R, :, :], in0=Os[0:R, :, 0:64],
                                                   in1=fac[0:R, 4:8].unsqueeze(2).broadcast_to([R, 4, 64]), op=ALU.mult),
                  reads=[Osn, 'fac'], writes=['tmpo'])
                E('dve', lambda e: e.tensor_tensor(out=acc[0:R, 4 * g:4 * g + 4, :], in0=acc[0:R, 4 * g:4 * g + 4, :], in1=tmpo[0:R, :, :], op=ALU.add),
                  reads=['acc', 'tmpo'], writes=['acc'])

            with (contextlib.ExitStack() if which == 'prompt' else contextlib.nullcontext()) as st:
              if which == 'prompt':
                  kcT = sb_t(st, "kcT", [64, 2, 64], BF16)
                  vc = sb_t(st, "vc", [64, 2, 64], BF16)
                  esel = sb_t(st, "esel", [32, T], BF16)
                  biasc = [sb_t(st, f"biasc{i}", [128, 512], F32) for i in range(2)]
                  selm = [sb_t(st, f"selm{i}", [128, 2, 32], F32) for i in range(2)]
                  load_const(esel[:], cd["esel_p"], 'esel', cast=True)
                  S.barrier()
                  with contextlib.ExitStack() as cs:
                      compress = make_compress(cs)
                      if 'cmpp' not in skip:
                          compress(lambda kind, g, l: XT[64 * g:64 * g + 64, kind, l:T:32], 64, kcT, lambda ct: vc, 'kcT', 'vc')
                      S.barrier()
                  step = 0
                  for qt in (range(NT) if 'nsap' not in skip else ()):
                      bp = qt % 2
                      E('sp', lambda e: e.dma_start(out=biasc[bp][:], in_=cd["biasc_p"][qt]), writes=['biasc'], dma=f'biasc{bp}')
                      E('sp', lambda e: e.dma_start(out=selm[bp][:, 0, :], in_=cd["mulm_p"][qt]), writes=['selm'], dma=f'selm{bp}')
                      E('sp', lambda e: e.dma_start(out=selm[bp][:, 1, :], in_=cd["addm_p"][qt]), writes=['selm'], dma=f'selm{bp}')
                      gate_ap = gates[:, qt, :].rearrange("p (h t) -> p h t", t=3)
                      nsa_compressed(128, 64, 32, 7, lambda h: QTb[0:64, qt, h, :], kcT, lambda ct: vc, biasc[bp], selm[bp][:, 0, :], selm[bp][:, 1, :],
                                     gate_ap, 'gates')
                      for g in range(2):
                          for (bi, KT, Vt, kts) in ((1, KTsel, Vsel, list(range(0, qt + 1))), (2, KTwin, Vwin, list(range(max(0, qt - 4), qt + 1)))):
                              Os = pf[4 + g][:, 0:260].rearrange("p (r c) -> p r c", r=4)
                              Osn = f'pf{4 + g}'
                              E('pe', lambda e: e.matmul(pf[4 + g][:, 0:260], zerom[:, :], zrhs[:, 0:260], start=True, stop=True, skip_group_check=True),
                                reads=['zerom', 'zrhs'], writes=[Osn])
                              for ki, kt in enumerate(kts):
                                  par = step % 2
                                  step += 1
                                  A, An = pf[par], f'pf{par}'
                                  Av = A[:, :].rearrange("p (r q) -> p r q", r=4)
                                  last = (bi == 2 and kt != qt and kt != qt - 4)
                                  E('pe', lambda e: e.matmul(Av, KT[0:68, g, kt * 128:(kt + 1) * 128], QTb[0:68, qt, 4 * g:4 * g + 4, :],
                                                             start=True, stop=last), reads=['KTx', 'QTb'], writes=[An])
                                  if bi == 1:
                                      E('pe', lambda e: e.matmul(Av, esel[0:32, kt * 128:(kt + 1) * 128],
                                                                 NEGT[0:32, g, :].unsqueeze(1).broadcast_to([32, 4, 128]),
                                                                 start=False, stop=(kt != qt)), reads=['esel', 'NEGT'], writes=[An])
                                  if kt == qt:
                                      E('pe', lambda e: e.matmul(Av, ident[:, :], cneg[:, :].unsqueeze(1).broadcast_to([128, 4, 128]),
                                                                 start=False, stop=True), reads=['ident', 'cneg'], writes=[An])
                                  elif bi == 2 and kt == qt - 4:
                                      E('pe', lambda e: e.matmul(Av, ident[:, :], bneg[:, :].unsqueeze(1).broadcast_to([128, 4, 128]),
                                                                 start=False, stop=True), reads=['ident', 'bneg'], writes=[An])
                                  E('act', lambda e: e.activation(out=Pex[par][:, :], in_=A[:, :], func=AF.Exp, scale=0.125),
                                    reads=[An], writes=[f'Pex{par}'])
                                  for r in range(4):
                                      E('pe', lambda e: e.matmul(Os[:, r, :], Pex[par][:, r * 128:(r + 1) * 128], Vt[:, kt, g, :],
                                                                 start=False, stop=True, skip_group_check=True),
                                        reads=[f'Pex{par}', 'Vx'], writes=[Osn])
                              branch_finish(128, g, Os, Osn, gate_ap, 'gates', bi)
                      E('dve', lambda e: e.tensor_tensor(out=ub[:, qt, :], in0=acc[:, :, :].rearrange("p h d -> p (h d)"), in1=gbp[:, qt, :], op=ALU.mult),
                        reads=['acc', 'gb'], writes=['ub'])
            S.barrier()

            with (contextlib.ExitStack() if which == 'sample' else contextlib.nullcontext()) as st:
              if which == 'sample':
                  XTs = sb_t(st, "XTs", [128, 2, PAST], BF16)
                  kcTs = sb_t(st, "kcTs", [64, 2, 256], BF16)
                  vcs = sb_t(st, "vcs", [128, 2, 2, 64], BF16)
                  esel_s = sb_t(st, "esel_s", [128, PAST], BF16)
                  aks = sb_t(st, "aks", [68, PAST], BF16)
                  biascs = sb_t(st, "biascs", [SQ, 8 * 256], F32)
                  selms = sb_t(st, "selms", [SQ, 2, 128], F32)
                  pg2 = [sb_t(st, f"pg2{i}", [128, 512], BF16) for i in range(2)]
                  vpg = [sb_t(st, f"vpg{i}", [128, 2, 65], BF16) for i in range(2)]
                  KTpg = [sb_t(st, f"KTpg{i}", [68, 2, 128], BF16) for i in range(2)]
                  swt = sb_t(st, "swt", [128, 4, 256], BF16)
                  vwt = sb_t(st, "vwt", [128, 4, 2, 65], BF16)
                  KTws = sb_t(st, "KTws", [68, 2, 512], BF16)
                  idxf = sb_t(st, "idxf2", [128, NPAGE], F32)
                  idxi = sb_t(st, "idxi2", [128, NPAGE], I32)
                  ptb = sb_t(st, "ptb2", [128, NPAGE], I32)
                  load_const(esel_s[:], cd["esel_s"], 'esel_s', cast=True)
                  load_const(aks[64:68, :], cd["ak"][:, 0:PAST], 'aks', cast=True)
                  load_const(biascs[:], cd["biasc_s"], 'biasc')
                  load_const(selms[:, 0, :], cd["mulm_s"], 'selm'); load_const(selms[:, 1, :], cd["addm_s"], 'selm')
                  for g in range(2):
                      load_const(KTws[64:68, g, :], cd["ak"][:, PAST - 512:PAST], 'KTws', cast=True)
                  E('dve', lambda e: e.memset(vpg[0][:, :, 64:65], 1.0), writes=['vpg0'])
                  E('dve', lambda e: e.memset(vpg[1][:, :, 64:65], 1.0), writes=['vpg1'])
                  E('dve', lambda e: e.memset(vwt[:, :, :, 64:65], 1.0), writes=['vwt'])
                  S.barrier()
                  compress = make_compress(st)
                  step = 0
                  for b in (range(SB) if 'nsas' not in skip else ()):
                      E('sp', lambda e: e.dma_start(out=ptb[:], in_=ptab[b:b + 1, :].partition_broadcast(128)), writes=['ptb2'], dma='ptb2')
                      E('dve', lambda e: e.tensor_copy(idxf[:], ptb[:]), reads=['ptb2'], writes=['idxf2'])
                      E('dve', lambda e: e.tensor_scalar(idxf[:], idxf[:], 128.0, iotap[:, 0:1], op0=ALU.mult, op1=ALU.add),
                        reads=['idxf2', 'iotap'], writes=['idxf2'])
                      E('dve', lambda e: e.tensor_copy(idxi[:], idxf[:]), reads=['idxf2'], writes=['idxi2'])
                      for pg in range(NPAGE):
                          par = pg % 2
                          for (cap, clo, cw) in [c_ for c_ in nsa_chunks if c_[1] < 256]:
                              E('pool', lambda e: e.indirect_dma_start(out=pg2[par][:, clo:clo + cw], out_offset=None, in_=cap,
                                                                       in_offset=bass.IndirectOffsetOnAxis(ap=idxi[:, pg:pg + 1], axis=0)),
                                reads=['idxi2', 'cache_nsa'], writes=[f'pg2{par}'], dma=f'pg2{par}')
                          pT = pb[par][:, :].rearrange("p (k c) -> p k c", k=8)
                          for kind in range(2):
                              E('pe', lambda e: e.transpose(pT[:, kind, :], pg2[par][:, kind * 128:(kind + 1) * 128], ident[:, :]),
                                reads=[f'pg2{par}', 'ident'], writes=[f'pb{par}'])
                          E('dve', lambda e: e.tensor_copy(XTs[:, :, pg * 128:(pg + 1) * 128], pT[:, 0:2, :]), reads=[f'pb{par}'], writes=['XTsrc'])
                      compress(lambda kind, g, l: XTs[64 * g:64 * g + 64, kind, l:PAST:32], 256, kcTs, lambda ct: vcs[:, ct, :, :], 'kcT', 'vc')
                      gate_ap = gates_s[:, b, :].rearrange("p (h t) -> p h t", t=3)
                      nsa_compressed(SQ, 256, 128, 6, lambda h: QTb_s[0:64, b, h, :], kcTs, lambda ct: vcs[:, ct, :, :], biascs, selms[:, 0, :], selms[:, 1, :],
                                     gate_ap, 'gates_s')
                      Osg = [pf[4][:, 0:260].rearrange("p (r c) -> p r c", r=4), pf[5][:, 0:260].rearrange("p (r c) -> p r c", r=4)]
                      par = step % 2
                      step += 1
                      A, An = pf[par], f'pf{par}'
                      for g in range(2):
                          Av = A[:, g * 32:(g + 1) * 32].rearrange("p (r q) -> p r q", r=4)
                          E('pe', lambda e: e.matmul(Av[0:8], KTsel_s[0:68, b, g, :], QTb_s[0:68, b, 4 * g:4 * g + 4, :], start=True, stop=False),
                            reads=['KTsel_s', 'QTb_s'], writes=[An])
                          E('pe', lambda e: e.matmul(Av[0:8], ident[0:8, 0:8], cneg8[0:8, :].unsqueeze(1).broadcast_to([8, 4, 8]), start=False, stop=True),
                            reads=['ident', 'cneg8'], writes=[An])
                      E('act', lambda e: e.activation(out=Pex[par][0:8, 0:64], in_=A[0:8, 0:64], func=AF.Exp, scale=0.125), reads=[An], writes=[f'Pex{par}'])
                      for g in range(2):
                          E('pe', lambda e: e.matmul(pf[4 + g][0:8, 0:260], zerom[:, 0:8], zrhs[:, 0:260], start=True, stop=True, skip_group_check=True),
                            reads=['zerom', 'zrhs'], writes=[f'pf{4 + g}'])
                          for r in range(4):
                              E('pe', lambda e: e.matmul(Osg[g][0:8, r, :], Pex[par][0:8, g * 32 + r * 8:g * 32 + r * 8 + 8], Vsel_s[0:8, b, g, :],
                                                         start=False, stop=True, skip_group_check=True),
                                reads=[f'Pex{par}', 'Vsel_s'], writes=[f'pf{4 + g}'])
                      for pg in range(NPAGE):
                          par = step % 2
                          step += 1
                          A, An = pf[par], f'pf{par}'
                          for (cap, clo, cw) in [c_ for c_ in nsa_chunks if c_[1] + c_[2] > 256]:
                              E('pool', lambda e: e.indirect_dma_start(out=pg2[par][:, clo:clo + cw], out_offset=None, in_=cap,
                                                                       in_offset=bass.IndirectOffsetOnAxis(ap=idxi[:, pg:pg + 1], axis=0)),
                                reads=['idxi2', 'cache_nsa'], writes=[f'pg2{par}'], dma=f'pg2{par}')
                          pT = pb[par][:, :].rearrange("p (k c) -> p k c", k=8)
                          for g in range(2):
                              E('pe', lambda e: e.transpose(pT[0:64, g, :], pg2[par][:, 256 + g * 64:256 + (g + 1) * 64], ident[:, :]),
                                reads=[f'pg2{par}', 'ident'], writes=[f'pb{par}'])
                          E('dve', lambda e: e.tensor_copy(KTpg[par][0:64, :, :], pT[0:64, 0:2, :]), reads=[f'pb{par}'], writes=[f'KTpg{par}'])
                          E('pool', lambda e: e.tensor_copy(KTpg[par][64:68, :, :],
                                                            aks[64:68, pg * 128:(pg + 1) * 128].unsqueeze(1).broadcast_to([4, 2, 128])),
                            reads=['aks'], writes=[f'KTpg{par}'])
                          E('dve', lambda e: e.tensor_copy(vpg[par][:, :, 0:64], pg2[par][:, 384:512].rearrange("p (g d) -> p g d", g=2)),
                            reads=[f'pg2{par}'], writes=[f'vpg{par}'])
                          for g in range(2):
                              Av = A[:, g * 32:(g + 1) * 32].rearrange("p (r q) -> p r q", r=4)
                              E('pe', lambda e: e.matmul(Av, KTpg[par][0:68, g, :], QTb_s[0:68, b, 4 * g:4 * g + 4, :], start=True, stop=False),
                                reads=[f'KTpg{par}', 'QTb_s'], writes=[An])
                              E('pe', lambda e: e.matmul(Av, esel_s[:, pg * 128:(pg + 1) * 128],
                                                         NEGT[:, g, 0:8].unsqueeze(1).broadcast_to([128, 4, 8]), start=False, stop=True),
                                reads=['esel_s', 'NEGT'], writes=[An])
                          E('act', lambda e: e.activation(out=Pex[par][:, 0:64], in_=A[:, 0:64], func=AF.Exp, scale=0.125), reads=[An], writes=[f'Pex{par}'])
                          for g in range(2):
                              for r in range(4):
                                  E('pe', lambda e: e.matmul(Osg[g][0:8, r, :], Pex[par][:, g * 32 + r * 8:g * 32 + r * 8 + 8], vpg[par][:, g, :],
                                                             start=False, stop=True, skip_group_check=True),
                                    reads=[f'Pex{par}', f'vpg{par}'], writes=[f'pf{4 + g}'])
                      for g in range(2):
                          branch_finish(SQ, g, Osg[g], f'pf{4 + g}', gate_ap, 'gates_s', 1)
                      E('pool', lambda e: e.dma_start(out=swt[:], in_=state_win[b].rearrange("(t p) c -> p t c", p=128)), writes=['swt'], dma='swt')
                      for t4 in range(4):
                          pT = pb[t4 % 2][:, :].rearrange("p (k c) -> p k c", k=8)
                          for g in range(2):
                              E('pe', lambda e: e.transpose(pT[0:64, g, :], swt[:, t4, g * 64:(g + 1) * 64], ident[:, :]),
                                reads=['swt', 'ident'], writes=[f'pb{t4 % 2}'])
                          E('dve', lambda e: e.tensor_copy(KTws[0:64, :, t4 * 128:(t4 + 1) * 128], pT[0:64, 0:2, :]), reads=[f'pb{t4 % 2}'], writes=['KTws'])
                      E('dve', lambda e: e.tensor_copy(vwt[:, :, :, 0:64], swt[:, :, 128:256].rearrange("p t (g d) -> p t g d", g=2)),
                        reads=['swt'], writes=['vwt'])
                      for g in range(2):
                          E('pe', lambda e: e.matmul(pf[4 + g][0:8, 0:260], zerom[:, 0:8], zrhs[:, 0:260], start=True, stop=True, skip_group_check=True),
                            reads=['zerom', 'zrhs'], writes=[f'pf{4 + g}'])
                      for t4 in range(5):
                          par = step % 2
                          step += 1
                          A, An = pf[par], f'pf{par}'
                          K_ = 128 if t4 < 4 else 8
                          for g in range(2):
                              Av = A[:, g * 32:(g + 1) * 32].rearrange("p (r q) -> p r q", r=4)
                              if t4 < 4:
                                  E('pe', lambda e: e.matmul(Av, KTws[0:68, g, t4 * 128:(t4 + 1) * 128], QTb_s[0:68, b, 4 * g:4 * g + 4, :],
                                                             start=True, stop=(t4 != 0)), reads=['KTws', 'QTb_s'], writes=[An])
                                  if t4 == 0:
                                      E('pe', lambda e: e.matmul(Av, ident[:, :], bneg8[:, :].unsqueeze(1).broadcast_to([128, 4, 8]), start=False, stop=True),
                                        reads=['ident', 'bneg8'], writes=[An])
                              else:
                                  E('pe', lambda e: e.matmul(Av[0:8], KTwin_s[0:68, b, g, :], QTb_s[0:68, b, 4 * g:4 * g + 4, :], start=True, stop=False),
                                    reads=['KTwin_s', 'QTb_s'], writes=[An])
                                  E('pe', lambda e: e.matmul(Av[0:8], ident[0:8, 0:8], cneg8[0:8, :].unsqueeze(1).broadcast_to([8, 4, 8]), start=False, stop=True),
                                    reads=['ident', 'cneg8'], writes=[An])
                          E('act', lambda e: e.activation(out=Pex[par][0:K_, 0:64], in_=A[0:K_, 0:64], func=AF.Exp, scale=0.125), reads=[An], writes=[f'Pex{par}'])
                          for g in range(2):
                              for r in range(4):
                                  vsrc = vwt[:, t4, g, :] if t4 < 4 else Vwin_s[0:8, b, g, :]
                                  E('pe', lambda e: e.matmul(Osg[g][0:8, r, :], Pex[par][0:K_, g * 32 + r * 8:g * 32 + r * 8 + 8], vsrc,
                                                             start=False, stop=True, skip_group_check=True),
                                    reads=[f'Pex{par}', 'vwt', 'Vwin_s'], writes=[f'pf{4 + g}'])
                      for g in range(2):
                          branch_finish(SQ, g, Osg[g], f'pf{4 + g}', gate_ap, 'gates_s', 2)
                      E('dve', lambda e: e.tensor_tensor(out=ub_s[:, b, :], in0=acc[0:SQ, :, :].rearrange("p h d -> p (h d)"), in1=gb_s[:, b, :], op=ALU.mult),
                        reads=['acc', 'gb_s'], writes=['ub_s'])
            S.barrier()
        nsa_phase('prompt')
        S.barrier()
        nsast.close()
        S.barrier()
        nsa_phase('sample')
        S.barrier()

        with contextlib.ExitStack() as st:
            wout = sb_t(st, "wout", [128, 8, D], BF16)
            wpg = sb_t(st, "wpg", [128, 8, D], BF16)
            wple = sb_t(st, "wple", [128, 2, D], BF16)
            load_const(wout[:], wout_d, 'wout', cast=True)
            load_const(wpg[:], wpg_d, 'wpg', cast=True)
            load_const(wple[:], wple_d, 'wple', cast=True)
            S.barrier()
            xt = [sb_t(st, f"oxt{i}", [128, D], F32) for i in range(2)]
            pt_ = [sb_t(st, f"opt{i}", [128, 256], F32) for i in range(2)]
            ptb_ = sb_t(st, "optb", [128, 256], BF16)
            uT = sb_t(st, "uT", [128, 8, 128], BF16)
            h32 = sb_t(st, "h32", [128, D], F32)
            hbf = sb_t(st, "hbf", [128, D], BF16)
            hT2 = sb_t(st, "hT2", [128, 8, 128], BF16)
            pT2 = sb_t(st, "pT2", [128, 2, 128], BF16)
            sg = sb_t(st, "sg", [128, D], F32)
            yo = [sb_t(st, f"yo{i}", [128, D], F32) for i in range(2)]
            if debug:
                ud = sb_t(st, "ud", [128, D], F32)
            for ti in (range(NT + SB) if 'out' not in skip else ()):
                is_p = ti < NT
                R = 128 if is_p else SQ
                b = ti - NT
                par = ti % 2
                r0 = ti * 128 if is_p else b * SQ
                ua_ap = ua[:, ti, :] if is_p else ua_s[:, b, :]
                ub_ap = ub[:, ti, :] if is_p else ub_s[:, b, :]
                xsrc = (x_p if is_p else x_s)[r0:r0 + R, :]
                psrc = (p_p if is_p else p_s)[r0:r0 + R, :]
                ydst = (y_p if is_p else y_s)[r0:r0 + R, :]
                E('sp', lambda e: e.dma_start(out=xt[par][0:R, :], in_=xsrc), writes=[f'oxt{par}'], dma=f'oxt{par}')
                E('sp', lambda e: e.dma_start(out=pt_[par][0:R, :], in_=psrc), writes=[f'opt{par}'], dma=f'opt{par}')
                if debug:
                    E('dve', lambda e: e.tensor_copy(ud[0:R, 0:512], ua_ap), reads=['ua', 'ua_s'], writes=['ud'])
                    E('dve', lambda e: e.tensor_copy(ud[0:R, 512:1024], ub_ap), reads=['ub', 'ub_s'], writes=['ud'])
                    E('pool', lambda e: e.dma_start(out=(u_pd if is_p else u_sd)[r0:r0 + R, :], in_=ud[0:R, :]), reads=['ud'], dma='o_ud')
                pT = pb[0][:, :].rearrange("p (k c) -> p k c", k=8)
                for k in range(8):
                    src = ua_ap[:, k * 128:(k + 1) * 128] if k < 4 else ub_ap[:, (k - 4) * 128:(k - 3) * 128]
                    E('pe', lambda e: e.transpose(pT[:, k, 0:R], src, ident[0:R, 0:R]), reads=['ua', 'ub', 'ua_s', 'ub_s', 'ident'], writes=['pb0'])
                E('act', lambda e: e.activation(out=uT[:, :, 0:R], in_=pT[:, :, 0:R], func=AF.Copy), reads=['pb0'], writes=['uT'])
                for half in range(2):
                    for k in range(8):
                        E('pe', lambda e: e.matmul(pf[half][0:R, :], uT[:, k, 0:R], wout[:, k, half * 512:(half + 1) * 512], start=(k == 0), stop=(k == 7)),
                          reads=['uT', 'wout'], writes=[f'pf{half}'])
                    E('dve', lambda e: e.tensor_tensor(out=h32[0:R, half * 512:(half + 1) * 512], in0=pf[half][0:R, :],
                                                       in1=xt[par][0:R, half * 512:(half + 1) * 512], op=ALU.add),
                      reads=[f'pf{half}', f'oxt{par}'], writes=['h32'])
                E('act', lambda e: e.activation(out=hbf[0:R, :], in_=h32[0:R, :], func=AF.Copy), reads=['h32'], writes=['hbf'])
                pTb = pb[1][:, :].rearrange("p (k c) -> p k c", k=8)
                for k in range(8):
                    E('pe', lambda e: e.transpose(pTb[:, k, 0:R], hbf[0:R, k * 128:(k + 1) * 128], ident[0:R, 0:R]), reads=['hbf', 'ident'], writes=['pb1'])
                E('act', lambda e: e.activation(out=hT2[:, :, 0:R], in_=pTb[:, :, 0:R], func=AF.Copy), reads=['pb1'], writes=['hT2'])
                E('dve', lambda e: e.tensor_copy(ptb_[0:R, :], pt_[par][0:R, :]), reads=[f'opt{par}'], writes=['optb'])
                for k in range(2):
                    E('pe', lambda e: e.transpose(pT[:, k, 0:R], ptb_[0:R, k * 128:(k + 1) * 128], ident[0:R, 0:R]), reads=['optb', 'ident'], writes=['pb0'])
                E('act', lambda e: e.activation(out=pT2[:, :, 0:R], in_=pT[:, 0:2, 0:R], func=AF.Copy), reads=['pb0'], writes=['pT2'])
                for half in range(2):
                    for k in range(8):
                        E('pe', lambda e: e.matmul(pf[2 + half][0:R, :], hT2[:, k, 0:R], wpg[:, k, half * 512:(half + 1) * 512], start=(k == 0), stop=(k == 7)),
                          reads=['hT2', 'wpg'], writes=[f'pf{2 + half}'])
                    for k in range(2):
                        E('pe', lambda e: e.matmul(pf[4 + half][0:R, :], pT2[:, k, 0:R], wple[:, k, half * 512:(half + 1) * 512], start=(k == 0), stop=(k == 1)),
                          reads=['pT2', 'wple'], writes=[f'pf{4 + half}'])
                    hs = slice(half * 512, (half + 1) * 512)
                    sigmoid_from(pf[2 + half][0:R, :], sg[0:R, hs], R, [f'pf{2 + half}'], 'sg')
                    E('dve', lambda e: e.tensor_tensor(out=sg[0:R, hs], in0=sg[0:R, hs], in1=pf[4 + half][0:R, :], op=ALU.mult),
                      reads=['sg', f'pf{4 + half}'], writes=['sg'])
                    E('dve', lambda e: e.tensor_tensor(out=yo[par][0:R, hs], in0=sg[0:R, hs], in1=h32[0:R, hs], op=ALU.add),
                      reads=['sg', 'h32'], writes=[f'yo{par}'])
                E('pool', lambda e: e.dma_start(out=ydst, in_=yo[par][0:R, :]), reads=[f'yo{par}'], dma=f'o_y{par}')
            for b in range(SB):
                E('pool', lambda e: e.dma_start(out=win_s[b, 0:504, :], in_=state_win[b, 8:512, :]), dma='o_winold')
            S.barrier()
        S.barrier()
    return nc


_NC_CACHE = {}


def _prep_shared(norm_g, w_in, q_norm_g, kcmp_norm_g, ksel_norm_g, kwin_norm_g,
                 cmp_pos_k, cmp_w1_k, cmp_w2_k, cmp_pos_v, cmp_w1_v, cmp_w2_v, w_out, w_ple, w_ple_gate):
    f = lambda a: np.ascontiguousarray(np.asarray(a, dtype=np.float32))
    bc = lambda v: f(np.broadcast_to(np.asarray(v)[0][None, :], (128, np.asarray(v).shape[1])))
    w1 = lambda w: f(np.concatenate([np.asarray(w)[0].reshape(32, 64, 256).transpose(1, 0, 2)] * 2, axis=0))
    w2 = lambda w: f(np.asarray(w)[0].reshape(2, 128, 64).transpose(1, 0, 2))
    shared = dict(
        g_b=bc(norm_g),
        w_in=f(np.asarray(w_in)[0].reshape(8, 128, N_IN).transpose(1, 0, 2)),
        qg_b=bc(q_norm_g), kcg_b=bc(kcmp_norm_g), ksg_b=bc(ksel_norm_g), kwg_b=bc(kwin_norm_g),
        w1k=w1(cmp_w1_k), w1v=w1(cmp_w1_v), w2k=w2(cmp_w2_k), w2v=w2(cmp_w2_v),
        pekT=f(np.asarray(cmp_pos_k)[0].T), pevT=f(np.asarray(cmp_pos_v)[0].T),
        w_out=f(np.asarray(w_out)[0].reshape(8, 128, D).transpose(1, 0, 2)),
        w_pg=f(np.asarray(w_ple_gate)[0].reshape(8, 128, D).transpose(1, 0, 2)),
        w_ple=f(np.asarray(w_ple)[0].reshape(2, 128, D).transpose(1, 0, 2)),
    )
    for k, v in host_consts().items():
        shared["c_" + k] = f(v)
    return shared


def _core_inputs(c, xp, xs, pp, ps, sw, pt):
    f = lambda a: np.ascontiguousarray(np.asarray(a, dtype=np.float32))
    m = {}
    m["x_p"] = f(xp[c]); m["p_p"] = f(pp[0, c])
    m["x_s"] = f(xs[SB * c:SB * c + SB].reshape(SB * SQ, D))
    m["p_s"] = f(ps[0, SB * c:SB * c + SB].reshape(SB * SQ, 256))
    m["state_win"] = f(sw[0, SB * c:SB * c + SB].reshape(SB, 512, 256))
    m["ptab"] = np.ascontiguousarray(pt[SB * c:SB * c + SB].astype(np.int32))
    return m


def kernel(x_prompt, x_sample, p_prompt, p_sample, cache_sb, cache_nsa, state_win, page_table,
           norm_g, w_in, q_norm_g, kcmp_norm_g, ksel_norm_g, kwin_norm_g,
           cmp_pos_k, cmp_w1_k, cmp_w2_k, cmp_pos_v, cmp_w1_v, cmp_w2_v,
           w_out, w_ple, w_ple_gate):
    f = lambda a: np.ascontiguousarray(np.asarray(a, dtype=np.float32))
    if "nc" not in _NC_CACHE:
        _NC_CACHE["nc"] = build()
    nc = _NC_CACHE["nc"]
    shared = _prep_shared(norm_g, w_in, q_norm_g, kcmp_norm_g, ksel_norm_g, kwin_norm_g,
                          cmp_pos_k, cmp_w1_k, cmp_w2_k, cmp_pos_v, cmp_w1_v, cmp_w2_v, w_out, w_ple, w_ple_gate)
    csb = np.asarray(cache_sb).reshape(NPOOL * 128, 1024)
    cns = np.asarray(cache_nsa).reshape(NPOOL * 128, 512)
    sh = NPOOL // NCORE * 128
    xp = np.asarray(x_prompt); xs = np.asarray(x_sample)
    pp = np.asarray(p_prompt); ps = np.asarray(p_sample)
    sw = np.asarray(state_win); pt = np.asarray(page_table)
    in_maps = []
    for c in range(NCORE):
        m = dict(shared)
        m.update(_core_inputs(c, xp, xs, pp, ps, sw, pt))
        m["csb_sh"] = f(csb[c * sh:(c + 1) * sh].reshape(sh, 8, 128).transpose(1, 0, 2))
        m["cns_sh"] = f(cns[c * sh:(c + 1) * sh].reshape(sh, 4, 128).transpose(1, 0, 2))
        in_maps.append(m)
    res = run_bass_kernel_spmd(nc, in_maps, core_ids=list(range(NCORE)))
    R = res.results
    cat = lambda k: np.stack([np.asarray(R[c][k]) for c in range(NCORE)])
    y_p = cat("y_p").reshape(8, T, D)
    y_s = cat("y_s").reshape(32, SQ, D)
    sbp = cat("sb_p").reshape(1, 8, T, 2, 8, 64)
    sbs = cat("sb_s").reshape(1, 32, SQ, 2, 8, 64)
    nsp = cat("nsa_p").reshape(1, 8, T, 4, 2, 64)
    nss = cat("nsa_s").reshape(1, 32, SQ, 4, 2, 64)
    wp = cat("win_p").reshape(1, 8, 512, 2, 2, 64)
    ws = cat("win_s").reshape(1, 32, 512, 2, 2, 64)
    return (y_p, y_s, sbp, sbs, nsp, nss, wp, ws)
```
